# Optimizing a Trainium2 kernel written in Bass

```python
import jax, jax.numpy as jnp
from jax import lax
import numpy as np

D_MODEL = 1024
BATCH = 16
SEQ = 2048
DEPTH = 1
DEC_BATCH = 8
DEC_SEQ = 32
PAST_LEN = 1024

CHUNK = 64
D_MIX = D_MODEL
D_POOL = D_MIX // 2
POOL_WINDOWS = (2, 4, 8, 16)
N_POOL_GROUPS = len(POOL_WINDOWS)
POOL_GROUP = D_POOL // N_POOL_GROUPS
POOL_BUF = max(POOL_WINDOWS) - 1
D_LRU = D_MIX - D_POOL
N_LRU_HEADS = 8
LRU_HEAD = D_LRU // N_LRU_HEADS
CONV_W = 4
LRU_C = 8.0
D_IN = D_POOL + 2 * D_LRU
PEER_HEADS = 8
N_KEYS = 128
N_EXPERTS = N_KEYS * N_KEYS
D_KEY = 256
D_HALF = D_KEY // 2
PEER_TOPK = 16
PEER_BLOCK = 256
EPS = 1e-6

kernel_name = 'hymba_pool_rglru_peer_stream_step'


def rmsnorm(x, g):
    xf = x.astype(jnp.float32)
    y = xf * lax.rsqrt(jnp.mean(xf * xf, axis=-1, keepdims=True) + EPS)
    return (y * g.astype(jnp.float32)).astype(x.dtype)


def pool_mixer(u, buf, start_pos, w_pool, pool_scale):
    T = u.shape[1]
    full = jnp.concatenate([buf.astype(u.dtype), u], axis=1)
    cs = jnp.cumsum(full.astype(jnp.float32), axis=1)
    cs = jnp.pad(cs, ((0, 0), (1, 0), (0, 0)))
    pos = start_pos + jnp.arange(T)
    outs = []
    for gi, w in enumerate(POOL_WINDOWS):
        lo, hi = gi * POOL_GROUP, (gi + 1) * POOL_GROUP
        csg = cs[..., lo:hi]
        s = csg[:, POOL_BUF + 1:POOL_BUF + 1 + T] - csg[:, POOL_BUF + 1 - w:POOL_BUF + 1 - w + T]
        cnt = jnp.minimum(w, pos + 1).astype(jnp.float32)[None, :, None]
        d = (s / cnt - u[..., lo:hi].astype(jnp.float32)).astype(u.dtype)
        outs.append(jnp.einsum('btc,cd->btd', d, w_pool[gi]))
    y = jnp.concatenate(outs, axis=-1) * pool_scale
    return y, full[:, -POOL_BUF:]


def causal_conv(xb, buf, w_conv, b_conv):
    T = xb.shape[1]
    full = jnp.concatenate([buf.astype(xb.dtype), xb], axis=1)
    y = b_conv
    for k in range(CONV_W):
        y = y + full[:, k:k + T] * w_conv[k]
    return y, full[:, -(CONV_W - 1):]


def rg_lru(xc, h0, w_a, b_a, w_x, b_x, lam):
    B, T, _ = xc.shape
    xh = xc.reshape(B, T, N_LRU_HEADS, LRU_HEAD)
    r = jax.nn.sigmoid(jnp.einsum('bthi,hij->bthj', xh, w_a).reshape(B, T, D_LRU) + b_a)
    ig = jax.nn.sigmoid(jnp.einsum('bthi,hij->bthj', xh, w_x).reshape(B, T, D_LRU) + b_x)
    log_a = -LRU_C * r.astype(jnp.float32) * jax.nn.softplus(-lam.astype(jnp.float32))
    a = jnp.exp(log_a)
    mult = jnp.sqrt(-jnp.expm1(2.0 * log_a))
    b = mult * (ig * xc).astype(jnp.float32)

    def step(h, ab):
        a_t, b_t = ab
        h = a_t * h + b_t
        return h, h

    hT, hs = lax.scan(step, h0.astype(jnp.float32), (jnp.swapaxes(a, 0, 1), jnp.swapaxes(b, 0, 1)))
    return jnp.swapaxes(hs, 0, 1).astype(xc.dtype), hT.astype(h0.dtype)


def peer(xn, w_q, key1, key2, w_u, w_v):
    B, T, D = xn.shape
    n = B * T
    blk = min(PEER_BLOCK, n)
    nb = -(-n // blk)
    xp = jnp.pad(xn.reshape(n, D), ((0, nb * blk - n), (0, 0))).reshape(nb, blk, D)

    def block(xb):
        q = (xb @ w_q).reshape(blk, PEER_HEADS, 2, D_HALF)
        s1 = jnp.einsum('thd,hkd->thk', q[:, :, 0], key1).astype(jnp.float32)
        s2 = jnp.einsum('thd,hkd->thk', q[:, :, 1], key2).astype(jnp.float32)
        v1, i1 = lax.top_k(s1, PEER_TOPK)
        v2, i2 = lax.top_k(s2, PEER_TOPK)
        cand_s = (v1[..., :, None] + v2[..., None, :]).reshape(blk, PEER_HEADS, PEER_TOPK * PEER_TOPK)
        cand_i = (i1[..., :, None] * N_KEYS + i2[..., None, :]).reshape(blk, PEER_HEADS, PEER_TOPK * PEER_TOPK)
        s, j = lax.top_k(cand_s, PEER_TOPK)
        idx = jnp.take_along_axis(cand_i, j, axis=-1)
        g = jax.nn.softmax(s, axis=-1)
        act = jax.nn.gelu(jnp.einsum('td,thkd->thk', xb, w_u[idx]).astype(jnp.float32))
        coef = (g * act).astype(xb.dtype)
        return jnp.einsum('thk,thkd->td', coef, w_v[idx])

    y = lax.map(block, xp).reshape(nb * blk, D)[:n]
    return y.reshape(B, T, D)


def trunk(x, pool_buf, conv_buf, h_state, start_pos, norm_mix, w_in, w_pool, pool_scale,
          conv_w, conv_b, lru_wa, lru_ba, lru_wx, lru_bx, lru_lambda, w_out, norm_ffn,
          peer_wq, peer_key1, peer_key2, peer_wu, peer_wv, norm_final):
    new_pool, new_conv, new_h = [], [], []
    for l in range(DEPTH):
        z = rmsnorm(x, norm_mix[l]) @ w_in[l]
        u_pool = z[..., :D_POOL]
        u_lru = z[..., D_POOL:D_POOL + D_LRU]
        gate = z[..., D_POOL + D_LRU:]
        y_pool, pb = pool_mixer(u_pool, pool_buf[l], start_pos, w_pool[l], pool_scale[l])
        xc, cb = causal_conv(u_lru, conv_buf[l], conv_w[l], conv_b[l])
        hs, hT = rg_lru(xc, h_state[l], lru_wa[l], lru_ba[l], lru_wx[l], lru_bx[l], lru_lambda[l])
        y_lru = hs * jax.nn.gelu(gate)
        x = x + jnp.concatenate([y_pool, y_lru], axis=-1) @ w_out[l]
        x = x + peer(rmsnorm(x, norm_ffn[l]), peer_wq[l], peer_key1[l], peer_key2[l], peer_wu[l], peer_wv[l])
        new_pool.append(pb)
        new_conv.append(cb)
        new_h.append(hT)
    return (rmsnorm(x, norm_final), jnp.stack(new_pool), jnp.stack(new_conv), jnp.stack(new_h))


def setup_inputs(seed: int = 0) -> dict:
    key = jax.random.key(seed)
    ks = jax.random.split(key, 32)
    f32 = jnp.float32
    nrm = lambda k, s, sc: jax.random.normal(k, s, f32) * sc
    lam_u = jax.random.uniform(ks[14], (DEPTH, D_LRU), f32, 0.9, 0.999)
    p = lam_u ** (1.0 / LRU_C)
    lru_lambda = jnp.log(p) - jnp.log1p(-p)
    return {
        'x_prompt': nrm(ks[0], (BATCH, SEQ, D_MODEL), 1.0),
        'x_sample': nrm(ks[1], (DEC_BATCH, DEC_SEQ, D_MODEL), 1.0),
        'cache_pool': nrm(ks[2], (DEPTH, DEC_BATCH, POOL_BUF, D_POOL), 1.0),
        'state_conv': nrm(ks[3], (DEPTH, DEC_BATCH, CONV_W - 1, D_LRU), 1.0),
        'state_lru': nrm(ks[4], (DEPTH, DEC_BATCH, D_LRU), 0.5),
        'norm_mix': 1.0 + nrm(ks[5], (DEPTH, D_MODEL), 0.02),
        'w_in': nrm(ks[6], (DEPTH, D_MODEL, D_IN), D_MODEL ** -0.5),
        'w_pool': nrm(ks[7], (DEPTH, N_POOL_GROUPS, POOL_GROUP, POOL_GROUP), POOL_GROUP ** -0.5),
        'pool_scale': 1.0 + nrm(ks[8], (DEPTH, D_POOL), 0.02),
        'conv_w': nrm(ks[9], (DEPTH, CONV_W, D_LRU), CONV_W ** -0.5),
        'conv_b': nrm(ks[10], (DEPTH, D_LRU), 0.01),
        'lru_wa': nrm(ks[11], (DEPTH, N_LRU_HEADS, LRU_HEAD, LRU_HEAD), LRU_HEAD ** -0.5),
        'lru_ba': nrm(ks[12], (DEPTH, D_LRU), 0.01),
        'lru_wx': nrm(ks[13], (DEPTH, N_LRU_HEADS, LRU_HEAD, LRU_HEAD), LRU_HEAD ** -0.5),
        'lru_bx': nrm(ks[15], (DEPTH, D_LRU), 0.01),
        'lru_lambda': lru_lambda,
        'w_out': nrm(ks[16], (DEPTH, D_MIX, D_MODEL), D_MIX ** -0.5),
        'norm_ffn': 1.0 + nrm(ks[17], (DEPTH, D_MODEL), 0.02),
        'peer_wq': nrm(ks[18], (DEPTH, D_MODEL, PEER_HEADS * D_KEY), D_MODEL ** -0.5),
        'peer_key1': nrm(ks[19], (DEPTH, PEER_HEADS, N_KEYS, D_HALF), D_HALF ** -0.5),
        'peer_key2': nrm(ks[20], (DEPTH, PEER_HEADS, N_KEYS, D_HALF), D_HALF ** -0.5),
        'peer_wu': nrm(ks[21], (DEPTH, N_EXPERTS, D_MODEL), D_MODEL ** -0.5),
        'peer_wv': nrm(ks[22], (DEPTH, N_EXPERTS, D_MODEL), (PEER_HEADS * PEER_TOPK) ** -0.5),
        'norm_final': 1.0 + nrm(ks[23], (D_MODEL,), 0.02),
    }


def reference(x_prompt, x_sample, cache_pool, state_conv, state_lru, norm_mix, w_in, w_pool,
              pool_scale, conv_w, conv_b, lru_wa, lru_ba, lru_wx, lru_bx, lru_lambda, w_out,
              norm_ffn, peer_wq, peer_key1, peer_key2, peer_wu, peer_wv, norm_final):
    dt = x_prompt.dtype
    pool0 = jnp.zeros((DEPTH, BATCH, POOL_BUF, D_POOL), dt)
    conv0 = jnp.zeros((DEPTH, BATCH, CONV_W - 1, D_LRU), dt)
    h0 = jnp.zeros((DEPTH, BATCH, D_LRU), state_lru.dtype)
    y_prompt, new_pool_prompt, new_conv_prompt, new_lru_prompt = trunk(
        x_prompt, pool0, conv0, h0, 0, norm_mix, w_in, w_pool, pool_scale, conv_w, conv_b,
        lru_wa, lru_ba, lru_wx, lru_bx, lru_lambda, w_out, norm_ffn, peer_wq, peer_key1,
        peer_key2, peer_wu, peer_wv, norm_final)
    y_sample, new_pool_sample, new_conv_sample, new_lru_sample = trunk(
        x_sample, cache_pool, state_conv, state_lru, PAST_LEN, norm_mix, w_in, w_pool, pool_scale,
        conv_w, conv_b, lru_wa, lru_ba, lru_wx, lru_bx, lru_lambda, w_out, norm_ffn, peer_wq,
        peer_key1, peer_key2, peer_wu, peer_wv, norm_final)
    return (y_prompt, y_sample, new_pool_prompt, new_conv_prompt, new_lru_prompt,
            new_pool_sample, new_conv_sample, new_lru_sample)
```

```python
import numpy as np
from contextlib import ExitStack
import concourse.bass as bass
import concourse.mybir as mybir
from concourse.bass_utils import run_bass_kernel_spmd

F32 = mybir.dt.float32
BF16 = mybir.dt.bfloat16
AF = mybir.ActivationFunctionType
ALU = mybir.AluOpType
AX = mybir.AxisListType

NCORES = 8
GELU_MODE = "tanh"
PIPE_V = False
INTERLEAVE = True
D = 1024
HP = 16
EPS = 1e-6


class Buf:
    def __init__(self, t, keys):
        self.t = t
        self.keys = list(keys)


def _flat(lst):
    out = []
    for x in lst:
        if isinstance(x, Buf):
            out.extend(x.keys)
        elif isinstance(x, list):
            out.extend(_flat(x))
        else:
            out.append(x)
    return out


class Prog:
    ENG = ("pe", "act", "dve", "pool", "sp")

    def __init__(self, nc):
        self.nc = nc
        self.stack = ExitStack()
        self.esem = {}
        self.dsem = {}
        self.ecnt = {e: 0 for e in self.ENG}
        self.dcnt = {}
        self.lastw = {}
        self.readers = {}
        self.seen = {e: {} for e in self.ENG}
        self.streams = {e: [] for e in self.ENG}
        for e in self.ENG:
            self.esem[e] = self.stack.enter_context(nc.semaphore("s_" + e))
        self.nops = 0
        self._rec = None

    def begin_record(self):
        self._rec = []

    def end_record(self):
        r = self._rec
        self._rec = None
        return r

    def replay(self, *lists):
        units = []
        for li, lst in enumerate(lists):
            n = len(lst)
            cur = None
            for i, it in enumerate(lst):
                is_pe = (it[0] == "op" and it[1] == "pe")
                if is_pe and cur is not None:
                    cur[3].append(it)
                else:
                    cur = ((i + 0.5) / n, li, i, [it])
                    units.append(cur)
                    if not is_pe:
                        cur = None
        units.sort(key=lambda x: (x[0], x[1], x[2]))
        for _, _, _, its in units:
            for it in its:
                if it[0] == "op":
                    self.op(*it[1:])
                else:
                    self.dma(*it[1:])

    def sb(self, name, shape, dt=F32):
        return self.stack.enter_context(self.nc.sbuf_tensor(name, list(shape), dt))

    def ps(self, name, shape, dt=F32):
        return self.stack.enter_context(self.nc.psum_tensor(name, list(shape), dt))

    def _need(self, eng, ev, waits):
        if ev is None:
            return
        kind, key, val = ev
        if kind == "E" and key == eng and eng == "pe":
            return
        semname = (kind, key)
        if self.seen[eng].get(semname, 0) >= val:
            return
        self.seen[eng][semname] = val
        waits.append((kind, key, val))

    def _deps(self, eng, reads, writes):
        waits = []
        for r in reads:
            self._need(eng, self.lastw.get(r), waits)
        for w in writes:
            lw = self.lastw.get(w)
            if not (lw is not None and lw[0] == "E" and lw[1] == eng and eng in ("dve", "act")):
                self._need(eng, lw, waits)
            for ev in self.readers.get(w, ()):
                if ev[0] == "E" and ev[1] == eng:
                    continue
                self._need(eng, ev, waits)
        return waits

    def _commit(self, ev, reads, writes):
        for w in writes:
            self.lastw[w] = ev
            self.readers[w] = []
        for r in reads:
            lst = self.readers.setdefault(r, [])
            lst[:] = [x for x in lst if (x[0], x[1]) != (ev[0], ev[1])]
            lst.append(ev)

    def op(self, eng, fn, reads=(), writes=()):
        if self._rec is not None:
            self._rec.append(("op", eng, fn, reads, writes))
            return
        reads = _flat(list(reads))
        writes = _flat(list(writes))
        waits = self._deps(eng, reads, writes)
        self.ecnt[eng] += 1
        ev = ("E", eng, self.ecnt[eng])
        self._commit(ev, reads, writes)
        self.streams[eng].append(("op", fn, waits))
        self.nops += 1

    def dma(self, q, semkey, fn, reads=(), writes=()):
        if self._rec is not None:
            self._rec.append(("dma", q, semkey, fn, reads, writes))
            return
        reads = _flat(list(reads))
        writes = _flat(list(writes))
        if semkey not in self.dsem:
            self.dsem[semkey] = self.stack.enter_context(
                self.nc.semaphore("d_%d" % len(self.dsem)))
            self.dcnt[semkey] = 0
        waits = self._deps(q, reads, writes)
        if self.dcnt[semkey] > 0:
            self._need(q, ("D", semkey, self.dcnt[semkey]), waits)
        self.dcnt[semkey] += 16
        ev = ("D", semkey, self.dcnt[semkey])
        self._commit(ev, reads, writes)
        self.streams[q].append(("dma", fn, waits, semkey))
        self.nops += 1

    def build(self):
        nc = self.nc
        fin = [("D", k, v) for k, v in self.dcnt.items()]
        self.streams["sp"].append(("fin", None, fin))

        def run(engname, eng):
            for item in self.streams[engname]:
                kind = item[0]
                for (k, key, val) in item[2]:
                    sem = self.esem[key] if k == "E" else self.dsem[key]
                    eng.wait_ge(sem, val)
                if kind == "op":
                    item[1](eng).then_inc(self.esem[engname], 1)
                elif kind == "dma":
                    item[1](eng).then_inc(self.dsem[item[3]], 16)

        with nc.Block() as block:
            @block.tensor
            def _(e):
                run("pe", e)

            @block.scalar
            def _(e):
                run("act", e)

            @block.vector
            def _(e):
                run("dve", e)

            @block.gpsimd
            def _(e):
                run("pool", e)

            @block.sync
            def _(e):
                run("sp", e)
        self.stack.close()


class Arena:
    def __init__(self, P, nslots):
        self.P = P
        self.n = nslots
        self.f32 = P.sb("arena", [128, nslots * 256], F32)
        self.b16 = self.f32.bitcast(BF16)
        self.ptr = 0

    def reset(self, at=0):
        self.ptr = at

    def get(self, shape, dt=F32):
        isz = 4 if dt == F32 else 2
        free = int(np.prod(shape[1:]))
        nbytes = free * isz
        ns = (nbytes + 1023) // 1024
        assert self.ptr + ns <= self.n, ("arena overflow", self.ptr, ns, self.n)
        s0 = self.ptr
        self.ptr += ns
        if dt == F32:
            ap = self.f32[:, s0 * 256: s0 * 256 + free]
        else:
            ap = self.b16[:, s0 * 512: s0 * 512 + free]
        if len(shape) == 3:
            ap = ap.rearrange("p (a b) -> p a b", a=shape[1])
        elif len(shape) == 4:
            ap = ap.rearrange("p (a b c) -> p a b c", a=shape[1], b=shape[2])
        return Buf(ap, [("ar", s) for s in range(s0, s0 + ns)])


def build_program(T_SEQ, NSEQ, T_S):
    nc = bass.Bass("TRN2", target_bir_lowering=False)
    P = Prog(nc)

    def din(name, shape):
        return nc.dram_tensor(name, list(shape), F32, kind="ExternalInput").ap()

    def dout(name, shape):
        return nc.dram_tensor(name, list(shape), F32, kind="ExternalOutput").ap()

    x_p = din("x_p", [NSEQ, T_SEQ, D])
    x_s = din("x_s", [T_S, D])
    cpool = din("cpool", [15, 512])
    cconv = din("cconv", [3, 512])
    clru = din("clru", [1, 512])
    norm_mix = din("norm_mix", [1, D])
    w_in = din("w_in", [D, 1536])
    w_pool = din("w_pool", [4, 128, 128])
    pool_scale = din("pool_scale", [1, 512])
    conv_w = din("conv_w", [4, 512])
    conv_b = din("conv_b", [1, 512])
    lru_wa = din("lru_wa", [8, 64, 64])
    lru_ba = din("lru_ba", [1, 512])
    lru_wx = din("lru_wx", [8, 64, 64])
    lru_bx = din("lru_bx", [1, 512])
    lru_lambda = din("lru_lambda", [1, 512])
    w_out = din("w_out", [D, D])
    norm_ffn = din("norm_ffn", [1, D])
    peer_wq = din("peer_wq", [D, 2048])
    key1 = din("key1", [8, 128, 128])
    key2 = din("key2", [8, 128, 128])
    w_u = din("w_u", [16384, D])
    w_v = din("w_v", [16384, D])
    norm_final = din("norm_final", [1, D])

    y_p = dout("y_p", [NSEQ, T_SEQ, D])
    y_s = dout("y_s", [T_S, D])
    npool_p = dout("npool_p", [NSEQ, 15, 512])
    nconv_p = dout("nconv_p", [NSEQ, 3, 512])
    nlru_p = dout("nlru_p", [NSEQ, 512])
    npool_s = dout("npool_s", [15, 512])
    nconv_s = dout("nconv_s", [3, 512])
    nlru_s = dout("nlru_s", [1, 512])

    wuT_d = nc.dram_tensor("wuT_d", [8, 128, 16384], BF16, kind="Internal").ap()
    wv_d = nc.dram_tensor("wv_d", [16384, D], BF16, kind="Internal").ap()

    w_in_sb = P.sb("w_in_sb", [128, 8, 1536], BF16)
    w_out_sb = P.sb("w_out_sb", [128, 8, 1024], BF16)
    w_q_sb = P.sb("w_q_sb", [128, 8, 2048], BF16)
    keyT_sb = P.sb("keyT_sb", [128, 16, 128], BF16)
    w_pool_sb = P.sb("w_pool_sb", [128, 4, 128], BF16)
    wa_sb = P.sb("wa_sb", [128, 4, 128], BF16)
    wx_sb = P.sb("wx_sb", [128, 4, 128], BF16)
    identf = P.sb("identf", [128, 128], F32)
    ident = P.sb("ident", [128, 128], BF16)
    vecs = P.sb("vecs", [128, 64], F32)
    nf_bc = P.sb("nf_bc", [128, D], F32)
    invc0 = P.sb("invc0", [128, 4, 128], F32)
    invc1 = P.sb("invc1", [128, 4, 128], F32)
    Us = [P.sb("U%d" % i, [128, 8, HP + 128], F32) for i in range(2)]
    hstates = [P.sb("hstate%d" % i, [128, 4], F32) for i in range(2)]
    stats = [P.sb("stat%d" % i, [128, 16], F32) for i in range(2)]
    stat2 = P.sb("stat2", [128, 8], F32)
    qTg = [P.sb("qTg%d" % i, [128, 16, 128], BF16) for i in range(2)]
    mxs = [P.sb("mxs%d" % i, [128, 16], F32) for i in range(2)]
    nshift = [P.sb("nshift%d" % i, [128, 8], F32) for i in range(2)]
    thrm = [P.sb("thrm%d" % i, [128, 8], F32) for i in range(2)]
    diag = [P.sb("diag%d" % i, [128, 8, 128], BF16) for i in range(2)]
    x1 = [P.sb("x1_%d" % i, [128, D], F32) for i in range(2)]
    xn2T = P.sb("xn2T", [128, 8, 256], BF16)
    bk = [P.ps("bk%d" % i, [128, 512], F32) for i in range(8)]
    BK = [[("bk", i, 0), ("bk", i, 1)] for i in range(8)]

    AR = Arena(P, 84)

    V_GMIX, V_GFFN, V_PSC, V_CW, V_CB, V_BA, V_BX, V_LAM, V_CNEG = 0, 8, 16, 20, 36, 40, 44, 48, 52

    def op(eng, fn, r, w):
        P.op(eng, fn, r, w)

    def mm(out, lhsT, rhs, start, stop, r, w):
        P.op("pe", lambda e: e.matmul(out, lhsT=lhsT, rhs=rhs, start=start, stop=stop), r, w)

    def act(out, in_, func, r, w, bias=None, scale=None, accum_out=None):
        kw = {}
        if bias is not None:
            kw["bias"] = bias
        if scale is not None:
            kw["scale"] = scale
        if accum_out is not None:
            kw["accum_out"] = accum_out
        P.op("act", lambda e: e.activation(out=out, in_=in_, func=func, **kw), r, w)

    def cp(eng, out, in_, r, w):
        if eng == "act":
            P.op("act", lambda e: e.copy(out=out, in_=in_), r, w)
        else:
            P.op(eng, lambda e: e.tensor_copy(out=out, in_=in_), r, w)

    def tt(eng, out, in0, in1, o, r, w):
        P.op(eng, lambda e: e.tensor_tensor(out=out, in0=in0, in1=in1, op=o), r, w)

    def ts(eng, out, in0, s1, s2, o0, o1, r, w, accum_out=None):
        if o1 is None:
            P.op(eng, lambda e: e.tensor_scalar(out=out, in0=in0, scalar1=s1, scalar2=None, op0=o0), r, w)
        else:
            P.op(eng, lambda e: e.tensor_scalar(out=out, in0=in0, scalar1=s1, scalar2=s2, op0=o0, op1=o1), r, w)

    def stt(eng, out, in0, scalar, in1, o0, o1, r, w, accum_out=None):
        if accum_out is None:
            P.op(eng, lambda e: e.scalar_tensor_tensor(out=out, in0=in0, scalar=scalar, in1=in1, op0=o0, op1=o1), r, w)
        else:
            P.op(eng, lambda e: e.scalar_tensor_tensor(out=out, in0=in0, scalar=scalar, in1=in1, op0=o0, op1=o1,
                                                       accum_out=accum_out), r, w)

    def memset(eng, ap, val, w):
        P.op(eng, lambda e: e.memset(ap, val), [], w)

    def dma(q, semkey, out, in_, r, w, nonc=False):
        if nonc:
            P.dma(q, semkey, lambda e: e.dma_start(out=out, in_=in_, allow_slow_non_contiguous=True), r, w)
        else:
            P.dma(q, semkey, lambda e: e.dma_start(out=out, in_=in_), r, w)

    def sbap(t, offset, dims):
        pst = t[:].ap[0][0]
        return bass.AP(tensor=t, offset=offset, ap=[[pst, 128]] + [list(d) for d in dims])

    memset("pool", identf[:], 1.0, ["identf"])
    op("pool", lambda e: e.affine_select(out=identf[:], in_=identf[:], pattern=[[-1, 128]],
                                         compare_op=ALU.is_equal, fill=0.0, base=0, channel_multiplier=1),
       ["identf"], ["identf"])
    cp("dve", ident[:], identf[:], ["identf"], ["ident"])

    def vload(col, src, n):
        dma("sp", "vec", vecs[:, col:col + n], src.rearrange("o (k p) -> p (o k)", p=128), [], ["vecs"], nonc=True)

    vload(V_GMIX, norm_mix, 8)
    vload(V_GFFN, norm_ffn, 8)
    vload(V_PSC, pool_scale, 4)
    for k in range(4):
        vload(V_CW + 4 * k, conv_w[k:k + 1, :], 4)
    vload(V_CB, conv_b, 4)
    vload(V_BA, lru_ba, 4)
    vload(V_BX, lru_bx, 4)
    vload(V_LAM, lru_lambda, 4)
    dma("sp", "vec", nf_bc[:], norm_final.partition_broadcast(128), [], ["nf_bc"])
    act(vecs[:, V_CNEG:V_CNEG + 4], vecs[:, V_LAM:V_LAM + 4], AF.Exp, ["vecs"], ["vecs"], scale=-1.0)
    act(vecs[:, V_CNEG:V_CNEG + 4], vecs[:, V_CNEG:V_CNEG + 4], AF.Ln, ["vecs"], ["vecs"], bias=1.0, scale=1.0)
    ts("dve", vecs[:, V_CNEG:V_CNEG + 4], vecs[:, V_CNEG:V_CNEG + 4], -8.0, None, ALU.mult, None, ["vecs"], ["vecs"])
    for g in range(4):
        w = 2 << g
        memset("pool", invc1[:, g, :], 1.0 / w, ["invc1"])
        memset("pool", invc0[:, g, :], 1.0 / w, ["invc0"])
        for t in range(w - 1):
            memset("pool", invc0[:, g, t:t + 1], 1.0 / (t + 1), ["invc0"])

    AR.reset()
    stg = [AR.get([128, 2048], F32) for _ in range(2)]
    nst = [0]

    def stage(src_ap, ncols, a=None):
        b = stg[nst[0] % 2]
        nst[0] += 1
        dst = b.t[:, 0:ncols]
        if a is not None:
            dst = dst.rearrange("p (a b) -> p a b", a=a)
        dma("sp", ("stg", nst[0] % 2), dst, src_ap, [], [b], nonc=True)
        return b

    for k in range(8):
        b = stage(w_in[k * 128:(k + 1) * 128, :], 1536)
        ts("dve", w_in_sb[:, k, :], b.t[:, 0:1536], vecs[:, V_GMIX + k:V_GMIX + k + 1], None, ALU.mult, None,
           [b, "vecs"], ["w_in_sb"])
        b = stage(w_out[k * 128:(k + 1) * 128, :], 1024)
        cp("act", w_out_sb[:, k, :], b.t[:, 0:1024], [b], ["w_out_sb"])
        b = stage(peer_wq[k * 128:(k + 1) * 128, :], 2048)
        ts("dve", w_q_sb[:, k, :], b.t[:, 0:2048], vecs[:, V_GFFN + k:V_GFFN + k + 1], None, ALU.mult, None,
           [b, "vecs"], ["w_q_sb"])
    b = stage(w_pool.rearrange("g c d -> c g d"), 512, a=4)
    cp("act", w_pool_sb[:], b.t[:, 0:512].rearrange("p (g d) -> p g d", g=4), [b], ["w_pool_sb"])
    for (src, dst, nm) in ((lru_wa, wa_sb, "wa_sb"), (lru_wx, wx_sb, "wx_sb")):
        b = stg[nst[0] % 2]
        nst[0] += 1
        memset("pool", b.t[:, 0:512], 0.0, [b])
        for c in range(4):
            dma("sp", ("stg", nst[0] % 2), b.t[0:64, c * 128:c * 128 + 64], src[2 * c], [], [b])
            dma("sp", ("stg", nst[0] % 2), b.t[64:128, c * 128 + 64:c * 128 + 128], src[2 * c + 1], [], [b])
        cp("act", dst[:], b.t[:, 0:512].rearrange("p (g d) -> p g d", g=4), [b], [nm])
    for half, ksrc in enumerate((key1, key2)):
        b = stage(ksrc.rearrange("h k d -> k h d"), 1024, a=8)
        kb = AR.get([128, 1024], BF16)
        cp("act", kb.t, b.t[:, 0:1024], [b], [kb])
        for h in range(8):
            m = 2 * h + half
            bi = h // 4
            mm(bk[bi][:, (h % 4) * 128:(h % 4 + 1) * 128], kb.t[:, h * 128:(h + 1) * 128], ident[:], True, True,
               [kb, "ident"], [BK[bi]])
        for bi in range(2):
            for q in range(4):
                h = bi * 4 + q
                cp("dve", keyT_sb[:, 2 * h + half, :], bk[bi][:, q * 128:(q + 1) * 128], [BK[bi]], ["keyT_sb"])

    AR.reset()
    stgu = [AR.get([128, 1024], F32) for _ in range(2)]
    wubf = [AR.get([128, 1024], BF16) for _ in range(2)]
    wuTs = [AR.get([128, 8, 128], BF16) for _ in range(2)]
    stgv = [AR.get([128, 1024], F32) for _ in range(2)]
    wvbf = [AR.get([128, 1024], BF16) for _ in range(2)]
    wuT_dv = wuT_d.rearrange("k d e -> d k e")
    def prep_load(eb):
        i = eb % 2
        rows = slice(eb * 128, (eb + 1) * 128)
        dma("sp", ("stgu", i), stgu[i].t, w_u[rows, :], [], [stgu[i]])
        dma("pool", ("stgv", i), stgv[i].t, w_v[rows, :], [], [stgv[i]])

    prep_load(0)
    for eb in range(128):
        i = eb % 2
        rows = slice(eb * 128, (eb + 1) * 128)
        cp("act", wubf[i].t, stgu[i].t, [stgu[i]], [wubf[i]])
        cp("act", wvbf[i].t, stgv[i].t, [stgv[i]], [wvbf[i]])
        if eb + 1 < 128:
            prep_load(eb + 1)
        for dk in range(8):
            bi = 2 * i + dk // 4
            mm(bk[bi][:, (dk % 4) * 128:(dk % 4 + 1) * 128], wubf[i].t[:, dk * 128:(dk + 1) * 128], ident[:], True, True,
               [wubf[i], "ident"], [BK[bi]])
        for hh in range(2):
            bi = 2 * i + hh
            cp("dve", wuTs[i].t[:, hh * 4:(hh + 1) * 4, :], bk[bi][:].rearrange("p (a b) -> p a b", a=4),
               [BK[bi]], [wuTs[i]])
        dma("sp", ("wuTs", i), wuT_dv[:, :, rows], wuTs[i].t, [wuTs[i]], [("wuTd", eb)], nonc=True)
        dma("pool", ("wvbf", i), wv_d[rows, :], wvbf[i].t, [wvbf[i]], [("wvd", eb)])

    NTS = T_SEQ // 128
    assert NSEQ == 2
    groups = [[("p", 0, j), ("p", 1, j)] for j in range(NTS)] + [[("s", 0, 0)]]

    xcnt = [0]
    split_idx = [0]

    def mixer_tile(kind, s, j, sub):
        first = (j == 0)
        last = (kind == "s") or (j == NTS - 1)
        TV = T_S if kind == "s" else 128
        bb = 4 * sub
        bkL = bk[bb:bb + 4]
        BKL = BK[bb:bb + 4]
        U = Us[sub]
        Uk = "U%d" % sub
        hstate = hstates[sub]
        Hk = "hstate%d" % sub
        stat = stats[sub]
        sk = lambda n: "st%d_%d" % (sub, n)
        XNk = "xn2T%d" % sub
        AR.reset(41 * sub)
        xt = AR.get([128, D], F32)
        junk = AR.get([128, D], BF16)
        xnb = AR.get([128, D], BF16)
        xnT = AR.get([128, 8, 128], BF16)
        s2 = AR.get([128, 4, HP + 128], F32)
        s4 = AR.get([128, 4, HP + 128], F32)
        s8 = AR.get([128, 4, HP + 128], F32)
        s16 = AR.get([128, 4, 128], F32)
        dbf = AR.get([128, 4, 128], BF16)
        xc = AR.get([128, 4, 128], F32)
        xcb = AR.get([128, 4, 128], BF16)
        rr = AR.get([128, 4, 128], F32)
        ig = AR.get([128, 4, 128], F32)
        aa = AR.get([128, 4, 128], F32)
        tA = AR.get([128, 4, 128], F32)
        bb = AR.get([128, 4, 128], F32)
        hs = AR.get([128, 4, 128], F32)
        gg = AR.get([128, 4, 128], F32)
        ymix = AR.get([128, 8, 128], BF16)
        AR.reset(41 * sub)
        e_t = AR.get([128, 16, 128], F32)
        sh = AR.get([128, 16, 128], F32)
        v16 = AR.get([128, 16, 16], F32)
        cand = AR.get([128, 8, 256], F32)
        tmp2 = AR.get([128, 8, 256], F32)
        c16 = AR.get([128, 8, 16], F32)

        xcnt[0] += 1
        if kind == "p":
            dma("sp", ("x", xcnt[0] % 2), xt.t, x_p[s, j * 128:(j + 1) * 128, :], [], [xt])
        else:
            memset("pool", xt.t, 0.0, [xt])
            dma("sp", ("x", xcnt[0] % 2), xt.t[0:T_S, :], x_s, [], [xt])
        act(junk.t, xt.t, AF.Square, [xt], [junk, sk(0)], accum_out=stat[:, 0:1])
        act(stat[:, 1:2], stat[:, 0:1], AF.Sqrt, [sk(0)], [sk(1)], bias=EPS, scale=1.0 / D)
        op("dve", lambda e: e.reciprocal(out=stat[:, 2:3], in_=stat[:, 1:2]), [sk(1)], [sk(2)])
        act(xnb.t, xt.t, AF.Identity, [xt, sk(2)], [xnb], scale=stat[:, 2:3])
        for k in range(8):
            bi = 0 if k < 4 else 1
            mm(bkL[bi][:, (k % 4) * 128:(k % 4 + 1) * 128], xnb.t[:, k * 128:(k + 1) * 128], ident[:], True, True,
               [xnb, "ident"], [BKL[bi]])
        cp("dve", xnT.t[:, 0:4, :], bkL[0][:].rearrange("p (a b) -> p a b", a=4), [BKL[0]], [xnT])
        cp("act", xnT.t[:, 4:8, :], bkL[1][:].rearrange("p (a b) -> p a b", a=4), [BKL[1]], [xnT])
        for m in range(12):
            bi = 1 + m // 4
            for k in range(8):
                mm(bkL[bi][:, (m % 4) * 128:(m % 4 + 1) * 128], w_in_sb[:, k, m * 128:(m + 1) * 128], xnT.t[:, k, :],
                   k == 0, k == 7, [xnT, "w_in_sb"], [BKL[bi]])
        if first and kind == "p":
            memset("pool", U[:, :, 0:HP], 0.0, [Uk])
        elif kind == "s":
            memset("pool", U[:, :, 0:HP], 0.0, [Uk])
            for g in range(4):
                dma("sp", "hist", U[:, g, 1:HP], cpool[:, g * 128:(g + 1) * 128].rearrange("r c -> c r"), [], [Uk], nonc=True)
                dma("sp", "hist", U[:, 4 + g, HP - 3:HP], cconv[:, g * 128:(g + 1) * 128].rearrange("r c -> c r"), [], [Uk], nonc=True)
            dma("sp", "hist", hstate[:], clru.rearrange("o (k p) -> p (o k)", p=128), [], [Hk], nonc=True)
        else:
            cp("pool", U[:, :, 0:HP], U[:, :, 128:128 + HP], [Uk], [Uk])
        if first and kind == "p":
            memset("pool", hstate[:], 0.0, [Hk])
        cp("dve", U[:, 0:4, HP:HP + 128], bkL[1][:].rearrange("p (a b) -> p a b", a=4), [BKL[1]], [Uk])
        cp("act", U[:, 4:8, HP:HP + 128], bkL[2][:].rearrange("p (a b) -> p a b", a=4), [BKL[2]], [Uk])
        act(gg.t, bkL[3][:].rearrange("p (a b) -> p a b", a=4), AF.Gelu_apprx_tanh, [BKL[3]], [gg])
        W = HP + 128
        tt("pool", s2.t[:, :, 1:W], U[:, 0:4, 1:W], U[:, 0:4, 0:W - 1], ALU.add, [Uk], [s2])
        tt("pool", s4.t[:, 1:4, 3:W], s2.t[:, 1:4, 3:W], s2.t[:, 1:4, 1:W - 2], ALU.add, [s2], [s4])
        tt("pool", s8.t[:, 2:4, 7:W], s4.t[:, 2:4, 7:W], s4.t[:, 2:4, 3:W - 4], ALU.add, [s4], [s8])
        tt("pool", s16.t[:, 3, :], s8.t[:, 3, HP:W], s8.t[:, 3, HP - 8:W - 8], ALU.add, [s8], [s16])
        cp("pool", s16.t[:, 0, :], s2.t[:, 0, HP:W], [s2], [s16])
        cp("pool", s16.t[:, 1, :], s4.t[:, 1, HP:W], [s4], [s16])
        cp("pool", s16.t[:, 2, :], s8.t[:, 2, HP:W], [s8], [s16])
        invc = invc0 if (first and kind == "p") else invc1
        tt("dve", s16.t, s16.t, invc[:], ALU.mult, [s16, "invc0", "invc1"], [s16])
        tt("dve", dbf.t, s16.t, U[:, 0:4, HP:W], ALU.subtract, [s16, Uk], [dbf])
        for g in range(4):
            mm(bkL[0][:, g * 128:(g + 1) * 128], w_pool_sb[:, g, :], dbf.t[:, g, :], True, True, [dbf, "w_pool_sb"], [BKL[0]])
        tt("dve", ymix.t[:, 0:4, :], bkL[0][:].rearrange("p (a b) -> p a b", a=4),
           sbap(vecs, V_PSC, [[1, 4], [0, 128]]), ALU.mult, [BKL[0], "vecs"], [ymix])
        cwc = lambda k, c: vecs[:, V_CW + 4 * k + c:V_CW + 4 * k + c + 1]
        for c in range(4):
            ts("dve", xc.t[:, c, :], U[:, 4 + c, HP:W], cwc(3, c), vecs[:, V_CB + c:V_CB + c + 1], ALU.mult, ALU.add,
               [Uk, "vecs"], [xc])
        for k in range(3):
            for c in range(4):
                stt("dve", xc.t[:, c, :], U[:, 4 + c, HP - 3 + k:W - 3 + k], cwc(k, c), xc.t[:, c, :], ALU.mult, ALU.add,
                    [Uk, "vecs", xc], [xc])
        cp("dve", xcb.t, xc.t, [xc], [xcb])
        for c in range(4):
            mm(bkL[1][:, c * 128:(c + 1) * 128], wa_sb[:, c, :], xcb.t[:, c, :], True, True, [xcb, "wa_sb"], [BKL[1]])
        for c in range(4):
            mm(bkL[2][:, c * 128:(c + 1) * 128], wx_sb[:, c, :], xcb.t[:, c, :], True, True, [xcb, "wx_sb"], [BKL[2]])
        for c in range(4):
            act(rr.t[:, c, :], bkL[1][:, c * 128:(c + 1) * 128], AF.Sigmoid, [BKL[1], "vecs"], [rr],
                bias=vecs[:, V_BA + c:V_BA + c + 1], scale=1.0)
        for c in range(4):
            act(ig.t[:, c, :], bkL[2][:, c * 128:(c + 1) * 128], AF.Sigmoid, [BKL[2], "vecs"], [ig],
                bias=vecs[:, V_BX + c:V_BX + c + 1], scale=1.0)
        for c in range(4):
            act(aa.t[:, c, :], rr.t[:, c, :], AF.Exp, [rr, "vecs"], [aa], scale=vecs[:, V_CNEG + c:V_CNEG + c + 1])
        tt("dve", tA.t, aa.t, aa.t, ALU.mult, [aa], [tA])
        ts("dve", tA.t, tA.t, -1.0, 1.0, ALU.mult, ALU.add, [tA], [tA])
        act(tA.t, tA.t, AF.Sqrt, [tA], [tA])
        tt("dve", bb.t, ig.t, xc.t, ALU.mult, [ig, xc], [bb])
        tt("dve", bb.t, bb.t, tA.t, ALU.mult, [bb, tA], [bb])
        for c in range(4):
            op("dve", lambda e, c=c: e.tensor_tensor_scan(out=hs.t[:, c, :], data0=aa.t[:, c, :], data1=bb.t[:, c, :],
                                                          initial=hstate[:, c:c + 1], op0=ALU.mult, op1=ALU.add),
               [aa, bb, Hk], [hs])
        cp("dve", hstate[:], hs.t[:, :, TV - 1], [hs], [Hk])
        tt("dve", ymix.t[:, 4:8, :], hs.t, gg.t, ALU.mult, [hs, gg], [ymix])
        if last:
            if kind == "p":
                d_pool, d_conv, d_lru = npool_p[s], nconv_p[s], nlru_p[s:s + 1, :]
            else:
                d_pool, d_conv, d_lru = npool_s, nconv_s, nlru_s
            for g in range(4):
                dma("sp", "stout", d_pool[:, g * 128:(g + 1) * 128].rearrange("r c -> c r"),
                    U[:, g, HP + TV - 15:HP + TV], [Uk], [], nonc=True)
                dma("sp", "stout", d_conv[:, g * 128:(g + 1) * 128].rearrange("r c -> c r"),
                    U[:, 4 + g, HP + TV - 3:HP + TV], [Uk], [], nonc=True)
            dma("sp", "stout", d_lru.rearrange("o (k p) -> p (o k)", p=128), hstate[:], [Hk], [], nonc=True)
        for half in range(2):
            bi = (0, 3)[half]
            for k in range(8):
                mm(bkL[bi][:], ymix.t[:, k, :], w_out_sb[:, k, half * 512:(half + 1) * 512], k == 0, k == 7,
                   [ymix, "w_out_sb"], [BKL[bi]])
        X1 = x1[sub]
        X1k = "x1_%d" % sub
        tt("dve", X1[:, 0:512], xt.t[:, 0:512], bkL[0][:], ALU.add, [xt, BKL[0]], [X1k])
        tt("dve", X1[:, 512:1024], xt.t[:, 512:1024], bkL[3][:], ALU.add, [xt, BKL[3]], [X1k])
        act(junk.t, X1[:], AF.Square, [X1k], [junk, sk(4)], accum_out=stat[:, 4:5])
        act(stat[:, 5:6], stat[:, 4:5], AF.Sqrt, [sk(4)], [sk(5)], bias=EPS, scale=1.0 / D)
        op("dve", lambda e: e.reciprocal(out=stat[:, 6:7], in_=stat[:, 5:6]), [sk(5)], [sk(6)])
        act(xnb.t, X1[:], AF.Identity, [X1k, sk(6)], [xnb], scale=stat[:, 6:7])
        for k in range(8):
            bi = 1 if k < 4 else 2
            mm(bkL[bi][:, (k % 4) * 128:(k % 4 + 1) * 128], xnb.t[:, k * 128:(k + 1) * 128], ident[:], True, True,
               [xnb, "ident"], [BKL[bi]])
        cp("dve", xn2T[:, 0:4, sub * 128:(sub + 1) * 128], bkL[1][:].rearrange("p (a b) -> p a b", a=4), [BKL[1]], [XNk])
        cp("act", xn2T[:, 4:8, sub * 128:(sub + 1) * 128], bkL[2][:].rearrange("p (a b) -> p a b", a=4), [BKL[2]], [XNk])
        split_idx[0] = len(P._rec)
        qb = [0, 1, 2, 3]
        for m in range(16):
            bi = qb[m // 4]
            for k in range(8):
                mm(bkL[bi][:, (m % 4) * 128:(m % 4 + 1) * 128], w_q_sb[:, k, m * 128:(m + 1) * 128],
                   xn2T[:, k, sub * 128:(sub + 1) * 128], k == 0, k == 7, [XNk, "w_q_sb"], [BKL[bi]])
        for qi, bi in enumerate(qb):
            cp("act" if qi % 2 else "dve", qTg[sub][:, qi * 4:(qi + 1) * 4, :], bkL[bi][:].rearrange("p (a b) -> p a b", a=4),
               [BKL[bi]], ["qTg%d" % sub])
        sbk = [0, 1, 2, 3]
        for m in range(16):
            bi = sbk[m // 4]
            mm(bkL[bi][:, (m % 4) * 128:(m % 4 + 1) * 128], qTg[sub][:, m, :], keyT_sb[:, m, :], True, True,
               ["qTg%d" % sub, "keyT_sb"], [BKL[bi]])
        E = e_t.t
        Ek = e_t
        MX = mxs[sub]
        MXk = "mxs%d" % sub
        for qi, bi in enumerate(sbk):
            op("dve", lambda e, qi=qi, bi=bi: e.tensor_reduce(out=MX[:, qi * 4:(qi + 1) * 4],
                                                               in_=bkL[bi][:].rearrange("p (a b) -> p a b", a=4),
                                                               axis=AX.X, op=ALU.max),
               [BKL[bi]], [MXk])
            tt("dve", sh.t[:, qi * 4:(qi + 1) * 4, :], bkL[bi][:].rearrange("p (a b) -> p a b", a=4),
               sbap(MX, qi * 4, [[1, 4], [0, 128]]), ALU.subtract, [BKL[bi], MXk], [sh])
        tt("dve", nshift[sub][:], sbap(MX, 0, [[2, 8]]), sbap(MX, 1, [[2, 8]]), ALU.add, [MXk], ["nshift%d" % sub])
        ts("dve", nshift[sub][:], nshift[sub][:], -1.0, None, ALU.mult, None, ["nshift%d" % sub], ["nshift%d" % sub])
        act(E, sh.t, AF.Exp, [sh], [Ek])
        for m in range(16):
            op("dve", lambda e, m=m: e.max(out=v16.t[:, m, 0:8], in_=E[:, m, :]), [Ek], [v16])
        for m in range(16):
            op("dve", lambda e, m=m: e.match_replace(out=sh.t[:, m, :], in_to_replace=v16.t[:, m, 0:8], in_values=E[:, m, :],
                                                    imm_value=-1.0), [Ek, v16], [sh])
        for m in range(16):
            op("dve", lambda e, m=m: e.max(out=v16.t[:, m, 8:16], in_=sh.t[:, m, :]), [sh], [v16])
        voff = v16.t.offset
        tt("dve", cand.t.rearrange("p h (a b) -> p h a b", a=16),
           sbap(AR.f32, voff, [[32, 8], [1, 16], [0, 16]]),
           sbap(AR.f32, voff + 16, [[32, 8], [0, 16], [1, 16]]), ALU.mult, [v16], [cand])
        for h in range(8):
            op("dve", lambda e, h=h: e.max(out=c16.t[:, h, 0:8], in_=cand.t[:, h, :]), [cand], [c16])
        for h in range(8):
            op("dve", lambda e, h=h: e.match_replace(out=tmp2.t[:, h, :], in_to_replace=c16.t[:, h, 0:8], in_values=cand.t[:, h, :],
                                                    imm_value=-1.0), [cand, c16], [tmp2])
        for h in range(8):
            op("dve", lambda e, h=h: e.max(out=c16.t[:, h, 8:16], in_=tmp2.t[:, h, :]), [tmp2], [c16])
        TH = thrm[sub]
        THk = "thrm%d" % sub
        ts("dve", TH[:], c16.t[:, :, 15], 1.0 - 1e-5, None, ALU.mult, None, [c16], [THk])
        tt("dve", tmp2.t, cand.t, sbap(TH, 0, [[1, 8], [0, 256]]), ALU.is_ge, [cand, THk], [tmp2])
        tt("dve", tmp2.t, tmp2.t, cand.t, ALU.mult, [tmp2, cand], [tmp2])
        op("dve", lambda e: e.tensor_reduce(out=stat[:, 8:16], in_=tmp2.t, axis=AX.X, op=ALU.add), [tmp2], [sk(8)])
        op("dve", lambda e: e.reciprocal(out=stat[:, 8:16], in_=stat[:, 8:16]), [sk(8)], [sk(8)])
        if GELU_MODE == "tanh":
            ts("dve", stat[:, 8:16], stat[:, 8:16], 0.5, None, ALU.mult, None, [sk(8)], [sk(8)])
        tt("dve", diag[sub][:], sbap(ident, 0, [[0, 8], [1, 128]]), sbap(stat, 8, [[1, 8], [0, 128]]), ALU.mult,
           ["ident", sk(8)], ["diag%d" % sub])

    def dense_group(grp):
        NS = len(grp)
        NT = NS * 128
        AR.reset()
        wuT_sb = [AR.get([128, 8, 512], BF16) for _ in range(2)]
        wv_sb = [AR.get([128, 4, 1024], BF16) for _ in range(2)]
        Eb = [AR.get([128, 4, 128], F32) for _ in range(4)]
        Mb = [[[AR.get([128, 4, 128], BF16) for h in range(8)] for sub in range(NS)] for _ in range(2)]
        gl = [AR.get([128, 256], F32) for _ in range(2)]
        AT = [AR.get([128, 256], BF16) for _ in range(2)]
        junk = AR.get([128, D], BF16)
        junk2 = AR.get([128, D], BF16)
        Kr = [AR.get([128, 4, 128], BF16) for _ in range(4)]
        cnt = [0]
        kcnt = [0]
        units = [(sub, h) for h in range(8) for sub in range(NS)]

        def load_w(cg):
            wi = cg % 2
            dma("sp", ("wuT_sb", wi), wuT_sb[wi].t, wuT_dv[:, :, cg * 512:(cg + 1) * 512],
                [("wuTd", eb) for eb in range(cg * 4, cg * 4 + 4)], [wuT_sb[wi]], nonc=True)
            dma("sp", ("wv_sb", wi), wv_sb[wi].t, wv_d[cg * 512:(cg + 1) * 512, :].rearrange("(j e) n -> e j n", e=128),
                [("wvd", eb) for eb in range(cg * 4, cg * 4 + 4)], [wv_sb[wi]])

        def emit_krep(slot):
            cg_, h_ = divmod(slot - 1, 8)
            if cg_ >= 32:
                return
            Kb_ = Kr[slot % 4]
            cp("act", Kb_.t, sbap(keyT_sb, (2 * h_) * 128 + cg_ * 4, [[1, 4], [0, 128]]), ["keyT_sb"], [Kb_])

        def mask_units(cg, ulist):
            MB = Mb[cg % 2]
            for (sub, h) in ulist:
                k = cnt[0]
                cnt[0] += 1
                lb = 6 + (k % 2)
                Ebuf = Eb[k % 4]
                qk = "qTg%d" % sub
                if sub == 0:
                    kcnt[0] += 1
                    emit_krep(kcnt[0] + 2)
                Kb = Kr[kcnt[0] % 4]
                mm(bk[lb][:], qTg[sub][:, 2 * h, :], Kb.t, True, False, [qk, Kb], [BK[lb]])
                mm(bk[lb][:], qTg[sub][:, 2 * h + 1, :], sbap(keyT_sb, (2 * h + 1) * 128, [[0, 4], [1, 128]]), False, True,
                   [qk, "keyT_sb"], [BK[lb]])
                act(Ebuf.t, bk[lb][:].rearrange("p (a b) -> p a b", a=4), AF.Exp, [BK[lb], "nshift%d" % sub], [Ebuf],
                    bias=nshift[sub][:, h:h + 1], scale=1.0)
                stt("dve", MB[sub][h].t, Ebuf.t, thrm[sub][:, h:h + 1], Ebuf.t, ALU.is_ge, ALU.mult,
                    [Ebuf, "thrm%d" % sub], [MB[sub][h]])

        C_G = 0.044715
        K_G = 0.7978845608028654

        def chunk_front(cg, j):
            wi = cg % 2
            MB = Mb[cg % 2]
            c = cg * 4 + j
            pi = c % 2
            UB = [BK[4][pi]]
            GB = [BK[5][pi]]
            ut = bk[4][:, pi * 256:pi * 256 + NT]
            g = gl[pi]
            gt_ = g.t[:, 0:NT]
            for dk in range(8):
                mm(ut, wuT_sb[wi].t[:, dk, j * 128:(j + 1) * 128], xn2T[:, dk, 0:NT], dk == 0, dk == 7,
                   [wuT_sb[wi], "xn2T0", "xn2T1"], UB)
            for sub in range(NS):
                for h in range(8):
                    mm(bk[5][:, pi * 256 + sub * 128:pi * 256 + (sub + 1) * 128], MB[sub][h].t[:, j, :], diag[sub][:, h, :],
                       h == 0, h == 7, [MB[sub][h], "diag%d" % sub], GB)
            if GELU_MODE == "tanh":
                act(gt_, ut, AF.Square, UB, [g], scale=float(np.sqrt(C_G)))
                stt("dve", gt_, gt_, 1.0, ut, ALU.add, ALU.mult, [g, UB], [g])
                act(gt_, gt_, AF.Tanh, [g], [g], scale=K_G)
                stt("dve", gt_, gt_, 1.0, ut, ALU.add, ALU.mult, [g, UB], [g])
            else:
                act(gt_, ut, AF.Gelu_apprx_tanh, UB, [g])
            tt("dve", AT[pi].t[:, 0:NT], gt_, bk[5][:, pi * 256:pi * 256 + NT], ALU.mult, [g, GB], [AT[pi]])

        def chunk_back(cg, j):
            wi = cg % 2
            c = cg * 4 + j
            pi = c % 2
            for sub in range(NS):
                for half in range(2):
                    yb = sub * 2 + half
                    mm(bk[yb][:], AT[pi].t[:, sub * 128:(sub + 1) * 128], wv_sb[wi].t[:, j, half * 512:(half + 1) * 512],
                       c == 0, c == 127, [AT[pi], wv_sb[wi]], [BK[yb]])

        load_w(0)
        emit_krep(1)
        emit_krep(2)
        mask_units(0, units)
        per = (len(units) + 3) // 4
        prev = None
        for cg in range(32):
            for j in range(4):
                chunk_front(cg, j)
                if PIPE_V:
                    if prev is not None:
                        chunk_back(*prev)
                    prev = (cg, j)
                    if j == 0 and cg + 1 < 32:
                        load_w(cg + 1)
                    if cg + 1 < 32:
                        mask_units(cg + 1, units[j * per:(j + 1) * per])
                else:
                    if cg + 1 < 32:
                        mask_units(cg + 1, units[j * per:(j + 1) * per])
                    chunk_back(cg, j)
                    if j == 0 and cg + 1 < 32:
                        load_w(cg + 1)
        if PIPE_V:
            chunk_back(*prev)
        jk = [junk, junk2]
        sc = [(4, 5, 6), (0, 1, 2)]
        subs = list(enumerate(grp))
        for sub, _ in subs:
            X1k = "x1_%d" % sub
            for half in range(2):
                tt("dve", x1[sub][:, half * 512:(half + 1) * 512], x1[sub][:, half * 512:(half + 1) * 512],
                   bk[sub * 2 + half][:], ALU.add, [X1k, BK[sub * 2 + half]], [X1k])
        for sub, _ in subs:
            a, b, c = sc[sub]
            act(jk[sub].t, x1[sub][:], AF.Square, ["x1_%d" % sub], [jk[sub], "ep%d" % a], accum_out=stat2[:, a:a + 1])
        for sub, _ in subs:
            a, b, c = sc[sub]
            act(stat2[:, b:b + 1], stat2[:, a:a + 1], AF.Sqrt, ["ep%d" % a], ["ep%d" % b], bias=EPS, scale=1.0 / D)
        for sub, _ in subs:
            a, b, c = sc[sub]
            op("dve", lambda e, b=b, c=c: e.reciprocal(out=stat2[:, c:c + 1], in_=stat2[:, b:b + 1]), ["ep%d" % b], ["ep%d" % c])
        for sub, (kind, s, j) in subs:
            a, b, c = sc[sub]
            X1k = "x1_%d" % sub
            stt("dve", x1[sub][:], x1[sub][:], stat2[:, c:c + 1], nf_bc[:], ALU.mult, ALU.mult, [X1k, "ep%d" % c, "nf_bc"], [X1k])
            if kind == "p":
                dma("sp", ("yout", sub), y_p[s, j * 128:(j + 1) * 128, :], x1[sub][:], [X1k], [])
            else:
                dma("sp", ("yout", sub), y_s, x1[sub][0:T_S, :], [X1k], [])

    for grp in groups:
        recs = []
        for sub, (kind, s, j) in enumerate(grp):
            P.begin_record()
            mixer_tile(kind, s, j, sub)
            r = P.end_record()
            recs.append((r[:split_idx[0]], r[split_idx[0]:]))
        P.replay(*[r[0] for r in recs])
        P.replay(*[r[1] for r in recs])
        dense_group(grp)

    P.build()
    return nc


_IN_NAMES = ["norm_mix", "w_in", "w_pool", "pool_scale", "conv_w", "conv_b", "lru_wa", "lru_ba", "lru_wx", "lru_bx",
             "lru_lambda", "w_out", "norm_ffn", "peer_wq"]


def run_cores(inputs, T_SEQ, NSEQ, T_S, ncores=NCORES):
    nc = build_program(T_SEQ, NSEQ, T_S)
    f = lambda a: np.ascontiguousarray(np.asarray(a, dtype=np.float32))
    shared = {}
    for n in _IN_NAMES:
        a = f(inputs[n])[0]
        if a.ndim == 1:
            a = a[None, :]
        shared[n] = np.ascontiguousarray(a)
    shared["key1"] = f(inputs["peer_key1"])[0]
    shared["key2"] = f(inputs["peer_key2"])[0]
    shared["w_u"] = f(inputs["peer_wu"])[0]
    shared["w_v"] = f(inputs["peer_wv"])[0]
    shared["norm_final"] = f(inputs["norm_final"])[None, :]
    xp = f(inputs["x_prompt"])
    xs = f(inputs["x_sample"])
    in_maps = []
    for c in range(ncores):
        m = dict(shared)
        m["x_p"] = np.ascontiguousarray(xp[c * NSEQ:(c + 1) * NSEQ])
        m["x_s"] = np.ascontiguousarray(xs[c])
        m["cpool"] = f(inputs["cache_pool"])[0, c]
        m["cconv"] = f(inputs["state_conv"])[0, c]
        m["clru"] = f(inputs["state_lru"])[0, c][None, :]
        in_maps.append(m)
    res = run_bass_kernel_spmd(nc, in_maps, core_ids=list(range(ncores)))
    R = res.results
    cat = lambda k: np.concatenate([np.asarray(r[k]) for r in R], axis=0)
    stk = lambda k: np.stack([np.asarray(r[k]) for r in R], axis=0)
    y_prompt = cat("y_p")
    y_sample = stk("y_s")
    return (y_prompt.astype(np.float32), y_sample.astype(np.float32),
            cat("npool_p")[None].astype(np.float32), cat("nconv_p")[None].astype(np.float32),
            cat("nlru_p")[None].astype(np.float32),
            stk("npool_s")[None].astype(np.float32), stk("nconv_s")[None].astype(np.float32),
            stk("nlru_s")[:, 0, :][None].astype(np.float32))


def kernel(**inputs):
    return run_cores(inputs, 2048, 2, 32)
```

```python
import numpy as np
from contextlib import ExitStack
import concourse.bass as bass
import concourse.mybir as mybir
from concourse.bass_utils import run_bass_kernel_spmd

F32 = mybir.dt.float32
BF16 = mybir.dt.bfloat16
AF = mybir.ActivationFunctionType
ALU = mybir.AluOpType
AX = mybir.AxisListType

NCORES = 8
GELU_MODE = "tanh"
PIPE_V = False
INTERLEAVE = True
D = 1024
HP = 16
EPS = 1e-6


class Buf:
    def __init__(self, t, keys):
        self.t = t
        self.keys = list(keys)


def _flat(lst):
    out = []
    for x in lst:
        if isinstance(x, Buf):
            out.extend(x.keys)
        elif isinstance(x, list):
            out.extend(_flat(x))
        else:
            out.append(x)
    return out


class Prog:
    ENG = ("pe", "act", "dve", "pool", "sp")

    def __init__(self, nc):
        self.nc = nc
        self.stack = ExitStack()
        self.esem = {}
        self.dsem = {}
        self.ecnt = {e: 0 for e in self.ENG}
        self.dcnt = {}
        self.lastw = {}
        self.readers = {}
        self.seen = {e: {} for e in self.ENG}
        self.streams = {e: [] for e in self.ENG}
        for e in self.ENG:
            self.esem[e] = self.stack.enter_context(nc.semaphore("s_" + e))
        self.nops = 0
        self._rec = None

    def begin_record(self):
        self._rec = []

    def end_record(self):
        r = self._rec
        self._rec = None
        return r

    def replay(self, *lists):
        units = []
        for li, lst in enumerate(lists):
            n = len(lst)
            cur = None
            for i, it in enumerate(lst):
                is_pe = (it[0] == "op" and it[1] == "pe")
                if is_pe and cur is not None:
                    cur[3].append(it)
                else:
                    cur = ((i + 0.5) / n, li, i, [it])
                    units.append(cur)
                if (not is_pe) or getattr(it[2], "_endgrp", True):
                    cur = None
        units.sort(key=lambda x: (x[0], x[1], x[2]))
        for _, _, _, its in units:
            for it in its:
                if it[0] == "op":
                    self.op(*it[1:])
                else:
                    self.dma(*it[1:])

    def sb(self, name, shape, dt=F32):
        return self.stack.enter_context(self.nc.sbuf_tensor(name, list(shape), dt))

    def ps(self, name, shape, dt=F32):
        return self.stack.enter_context(self.nc.psum_tensor(name, list(shape), dt))

    def _need(self, eng, ev, waits):
        if ev is None:
            return
        kind, key, val = ev
        if kind == "E" and key == eng and eng == "pe":
            return
        semname = (kind, key)
        if self.seen[eng].get(semname, 0) >= val:
            return
        self.seen[eng][semname] = val
        waits.append((kind, key, val))

    def _deps(self, eng, reads, writes):
        waits = []
        for r in reads:
            self._need(eng, self.lastw.get(r), waits)
        for w in writes:
            lw = self.lastw.get(w)
            if not (lw is not None and lw[0] == "E" and lw[1] == eng and eng in ("dve", "act")):
                self._need(eng, lw, waits)
            for ev in self.readers.get(w, ()):
                if ev[0] == "E" and ev[1] == eng:
                    continue
                self._need(eng, ev, waits)
        return waits

    def _commit(self, ev, reads, writes):
        for w in writes:
            self.lastw[w] = ev
            self.readers[w] = []
        for r in reads:
            lst = self.readers.setdefault(r, [])
            lst[:] = [x for x in lst if (x[0], x[1]) != (ev[0], ev[1])]
            lst.append(ev)

    def op(self, eng, fn, reads=(), writes=()):
        if self._rec is not None:
            self._rec.append(("op", eng, fn, reads, writes))
            return
        reads = _flat(list(reads))
        writes = _flat(list(writes))
        waits = self._deps(eng, reads, writes)
        self.ecnt[eng] += 1
        ev = ("E", eng, self.ecnt[eng])
        self._commit(ev, reads, writes)
        self.streams[eng].append(("op", fn, waits))
        self.nops += 1

    def dma(self, q, semkey, fn, reads=(), writes=()):
        if self._rec is not None:
            self._rec.append(("dma", q, semkey, fn, reads, writes))
            return
        reads = _flat(list(reads))
        writes = _flat(list(writes))
        if semkey not in self.dsem:
            self.dsem[semkey] = self.stack.enter_context(
                self.nc.semaphore("d_%d" % len(self.dsem)))
            self.dcnt[semkey] = 0
        waits = self._deps(q, reads, writes)
        if self.dcnt[semkey] > 0:
            self._need(q, ("D", semkey, self.dcnt[semkey]), waits)
        self.dcnt[semkey] += 16
        ev = ("D", semkey, self.dcnt[semkey])
        self._commit(ev, reads, writes)
        self.streams[q].append(("dma", fn, waits, semkey))
        self.nops += 1

    def build(self):
        nc = self.nc
        fin = [("D", k, v) for k, v in self.dcnt.items()]
        self.streams["sp"].append(("fin", None, fin))

        def run(engname, eng):
            for item in self.streams[engname]:
                kind = item[0]
                for (k, key, val) in item[2]:
                    sem = self.esem[key] if k == "E" else self.dsem[key]
                    eng.wait_ge(sem, val)
                if kind == "op":
                    item[1](eng).then_inc(self.esem[engname], 1)
                elif kind == "dma":
                    item[1](eng).then_inc(self.dsem[item[3]], 16)

        with nc.Block() as block:
            @block.tensor
            def _(e):
                run("pe", e)

            @block.scalar
            def _(e):
                run("act", e)

            @block.vector
            def _(e):
                run("dve", e)

            @block.gpsimd
            def _(e):
                run("pool", e)

            @block.sync
            def _(e):
                run("sp", e)
        self.stack.close()


class Arena:
    def __init__(self, P, nslots):
        self.P = P
        self.n = nslots
        self.f32 = P.sb("arena", [128, nslots * 256], F32)
        self.b16 = self.f32.bitcast(BF16)
        self.ptr = 0

    def reset(self, at=0):
        self.ptr = at

    def get(self, shape, dt=F32):
        isz = 4 if dt == F32 else 2
        free = int(np.prod(shape[1:]))
        nbytes = free * isz
        ns = (nbytes + 1023) // 1024
        assert self.ptr + ns <= self.n, ("arena overflow", self.ptr, ns, self.n)
        s0 = self.ptr
        self.ptr += ns
        if dt == F32:
            ap = self.f32[:, s0 * 256: s0 * 256 + free]
        else:
            ap = self.b16[:, s0 * 512: s0 * 512 + free]
        if len(shape) == 3:
            ap = ap.rearrange("p (a b) -> p a b", a=shape[1])
        elif len(shape) == 4:
            ap = ap.rearrange("p (a b c) -> p a b c", a=shape[1], b=shape[2])
        return Buf(ap, [("ar", s) for s in range(s0, s0 + ns)])


def build_program(T_SEQ, NSEQ, T_S):
    nc = bass.Bass("TRN2", target_bir_lowering=False)
    P = Prog(nc)

    def din(name, shape):
        return nc.dram_tensor(name, list(shape), F32, kind="ExternalInput").ap()

    def dout(name, shape):
        return nc.dram_tensor(name, list(shape), F32, kind="ExternalOutput").ap()

    x_p = din("x_p", [NSEQ, T_SEQ, D])
    x_s = din("x_s", [T_S, D])
    cpool = din("cpool", [15, 512])
    cconv = din("cconv", [3, 512])
    clru = din("clru", [1, 512])
    norm_mix = din("norm_mix", [1, D])
    w_in = din("w_in", [D, 1536])
    w_pool = din("w_pool", [4, 128, 128])
    pool_scale = din("pool_scale", [1, 512])
    conv_w = din("conv_w", [4, 512])
    conv_b = din("conv_b", [1, 512])
    lru_wa = din("lru_wa", [8, 64, 64])
    lru_ba = din("lru_ba", [1, 512])
    lru_wx = din("lru_wx", [8, 64, 64])
    lru_bx = din("lru_bx", [1, 512])
    lru_lambda = din("lru_lambda", [1, 512])
    w_out = din("w_out", [D, D])
    norm_ffn = din("norm_ffn", [1, D])
    peer_wq = din("peer_wq", [D, 2048])
    key1 = din("key1", [8, 128, 128])
    key2 = din("key2", [8, 128, 128])
    w_u = din("w_u", [16384, D])
    w_v = din("w_v", [16384, D])
    norm_final = din("norm_final", [1, D])

    y_p = dout("y_p", [NSEQ, T_SEQ, D])
    y_s = dout("y_s", [T_S, D])
    npool_p = dout("npool_p", [NSEQ, 15, 512])
    nconv_p = dout("nconv_p", [NSEQ, 3, 512])
    nlru_p = dout("nlru_p", [NSEQ, 512])
    npool_s = dout("npool_s", [15, 512])
    nconv_s = dout("nconv_s", [3, 512])
    nlru_s = dout("nlru_s", [1, 512])

    wuT_d = nc.dram_tensor("wuT_d", [8, 128, 16384], BF16, kind="Internal").ap()
    wv_d = nc.dram_tensor("wv_d", [16384, D], BF16, kind="Internal").ap()

    w_in_sb = P.sb("w_in_sb", [128, 8, 1536], BF16)
    w_out_sb = P.sb("w_out_sb", [128, 8, 1024], BF16)
    w_q_sb = P.sb("w_q_sb", [128, 8, 2048], BF16)
    keyT_sb = P.sb("keyT_sb", [128, 16, 128], BF16)
    w_pool_sb = P.sb("w_pool_sb", [128, 4, 128], BF16)
    wa_sb = P.sb("wa_sb", [128, 4, 128], BF16)
    wx_sb = P.sb("wx_sb", [128, 4, 128], BF16)
    identf = P.sb("identf", [128, 128], F32)
    ident = P.sb("ident", [128, 128], BF16)
    vecs = P.sb("vecs", [128, 64], F32)
    nf_bc = P.sb("nf_bc", [128, D], F32)
    invc0 = P.sb("invc0", [128, 4, 128], F32)
    invc1 = P.sb("invc1", [128, 4, 128], F32)
    Us = [P.sb("U%d" % i, [128, 8, HP + 128], F32) for i in range(2)]
    hstates = [P.sb("hstate%d" % i, [128, 4], F32) for i in range(2)]
    stats = [P.sb("stat%d" % i, [128, 16], F32) for i in range(2)]
    stat2 = P.sb("stat2", [128, 8], F32)
    qTg = [P.sb("qTg%d" % i, [128, 16, 128], BF16) for i in range(2)]
    mxs = [P.sb("mxs%d" % i, [128, 16], F32) for i in range(2)]
    nshift = [P.sb("nshift%d" % i, [128, 8], F32) for i in range(2)]
    thrm = [P.sb("thrm%d" % i, [128, 8], F32) for i in range(2)]
    diag = [P.sb("diag%d" % i, [128, 8, 128], BF16) for i in range(2)]
    x1 = [P.sb("x1_%d" % i, [128, D], F32) for i in range(2)]
    xn2T = P.sb("xn2T", [128, 8, 256], BF16)
    bk = [P.ps("bk%d" % i, [128, 512], F32) for i in range(8)]
    BK = [[("bk", i, 0), ("bk", i, 1)] for i in range(8)]

    AR = Arena(P, 84)

    V_GMIX, V_GFFN, V_PSC, V_CW, V_CB, V_BA, V_BX, V_LAM, V_CNEG = 0, 8, 16, 20, 36, 40, 44, 48, 52

    def op(eng, fn, r, w):
        P.op(eng, fn, r, w)

    def mm(out, lhsT, rhs, start, stop, r, w):
        fn = lambda e: e.matmul(out, lhsT=lhsT, rhs=rhs, start=start, stop=stop)
        fn._endgrp = bool(stop)
        P.op("pe", fn, r, w)

    def act(out, in_, func, r, w, bias=None, scale=None, accum_out=None):
        kw = {}
        if bias is not None:
            kw["bias"] = bias
        if scale is not None:
            kw["scale"] = scale
        if accum_out is not None:
            kw["accum_out"] = accum_out
        P.op("act", lambda e: e.activation(out=out, in_=in_, func=func, **kw), r, w)

    def cp(eng, out, in_, r, w):
        if eng == "act":
            P.op("act", lambda e: e.copy(out=out, in_=in_), r, w)
        else:
            P.op(eng, lambda e: e.tensor_copy(out=out, in_=in_), r, w)

    def tt(eng, out, in0, in1, o, r, w):
        P.op(eng, lambda e: e.tensor_tensor(out=out, in0=in0, in1=in1, op=o), r, w)

    def ts(eng, out, in0, s1, s2, o0, o1, r, w, accum_out=None):
        if o1 is None:
            P.op(eng, lambda e: e.tensor_scalar(out=out, in0=in0, scalar1=s1, scalar2=None, op0=o0), r, w)
        else:
            P.op(eng, lambda e: e.tensor_scalar(out=out, in0=in0, scalar1=s1, scalar2=s2, op0=o0, op1=o1), r, w)

    def stt(eng, out, in0, scalar, in1, o0, o1, r, w, accum_out=None):
        if accum_out is None:
            P.op(eng, lambda e: e.scalar_tensor_tensor(out=out, in0=in0, scalar=scalar, in1=in1, op0=o0, op1=o1), r, w)
        else:
            P.op(eng, lambda e: e.scalar_tensor_tensor(out=out, in0=in0, scalar=scalar, in1=in1, op0=o0, op1=o1,
                                                       accum_out=accum_out), r, w)

    def memset(eng, ap, val, w):
        P.op(eng, lambda e: e.memset(ap, val), [], w)

    def dma(q, semkey, out, in_, r, w, nonc=False):
        if nonc:
            P.dma(q, semkey, lambda e: e.dma_start(out=out, in_=in_, allow_slow_non_contiguous=True), r, w)
        else:
            P.dma(q, semkey, lambda e: e.dma_start(out=out, in_=in_), r, w)

    def sbap(t, offset, dims):
        pst = t[:].ap[0][0]
        return bass.AP(tensor=t, offset=offset, ap=[[pst, 128]] + [list(d) for d in dims])

    memset("pool", identf[:], 1.0, ["identf"])
    op("pool", lambda e: e.affine_select(out=identf[:], in_=identf[:], pattern=[[-1, 128]],
                                         compare_op=ALU.is_equal, fill=0.0, base=0, channel_multiplier=1),
       ["identf"], ["identf"])
    cp("dve", ident[:], identf[:], ["identf"], ["ident"])

    def vload(col, src, n):
        dma("sp", "vec", vecs[:, col:col + n], src.rearrange("o (k p) -> p (o k)", p=128), [], ["vecs"], nonc=True)

    vload(V_GMIX, norm_mix, 8)
    vload(V_GFFN, norm_ffn, 8)
    vload(V_PSC, pool_scale, 4)
    for k in range(4):
        vload(V_CW + 4 * k, conv_w[k:k + 1, :], 4)
    vload(V_CB, conv_b, 4)
    vload(V_BA, lru_ba, 4)
    vload(V_BX, lru_bx, 4)
    vload(V_LAM, lru_lambda, 4)
    dma("sp", "vec", nf_bc[:], norm_final.partition_broadcast(128), [], ["nf_bc"])
    act(vecs[:, V_CNEG:V_CNEG + 4], vecs[:, V_LAM:V_LAM + 4], AF.Exp, ["vecs"], ["vecs"], scale=-1.0)
    act(vecs[:, V_CNEG:V_CNEG + 4], vecs[:, V_CNEG:V_CNEG + 4], AF.Ln, ["vecs"], ["vecs"], bias=1.0, scale=1.0)
    ts("dve", vecs[:, V_CNEG:V_CNEG + 4], vecs[:, V_CNEG:V_CNEG + 4], -8.0, None, ALU.mult, None, ["vecs"], ["vecs"])
    for g in range(4):
        w = 2 << g
        memset("pool", invc1[:, g, :], 1.0 / w, ["invc1"])
        memset("pool", invc0[:, g, :], 1.0 / w, ["invc0"])
        for t in range(w - 1):
            memset("pool", invc0[:, g, t:t + 1], 1.0 / (t + 1), ["invc0"])

    AR.reset()
    stg = [AR.get([128, 2048], F32) for _ in range(2)]
    nst = [0]

    def stage(src_ap, ncols, a=None):
        b = stg[nst[0] % 2]
        nst[0] += 1
        dst = b.t[:, 0:ncols]
        if a is not None:
            dst = dst.rearrange("p (a b) -> p a b", a=a)
        dma("sp", ("stg", nst[0] % 2), dst, src_ap, [], [b], nonc=True)
        return b

    for k in range(8):
        b = stage(w_in[k * 128:(k + 1) * 128, :], 1536)
        ts("dve", w_in_sb[:, k, :], b.t[:, 0:1536], vecs[:, V_GMIX + k:V_GMIX + k + 1], None, ALU.mult, None,
           [b, "vecs"], ["w_in_sb"])
        b = stage(w_out[k * 128:(k + 1) * 128, :], 1024)
        cp("act", w_out_sb[:, k, :], b.t[:, 0:1024], [b], ["w_out_sb"])
        b = stage(peer_wq[k * 128:(k + 1) * 128, :], 2048)
        ts("dve", w_q_sb[:, k, :], b.t[:, 0:2048], vecs[:, V_GFFN + k:V_GFFN + k + 1], None, ALU.mult, None,
           [b, "vecs"], ["w_q_sb"])
    b = stage(w_pool.rearrange("g c d -> c g d"), 512, a=4)
    cp("act", w_pool_sb[:], b.t[:, 0:512].rearrange("p (g d) -> p g d", g=4), [b], ["w_pool_sb"])
    for (src, dst, nm) in ((lru_wa, wa_sb, "wa_sb"), (lru_wx, wx_sb, "wx_sb")):
        b = stg[nst[0] % 2]
        nst[0] += 1
        memset("pool", b.t[:, 0:512], 0.0, [b])
        for c in range(4):
            dma("sp", ("stg", nst[0] % 2), b.t[0:64, c * 128:c * 128 + 64], src[2 * c], [], [b])
            dma("sp", ("stg", nst[0] % 2), b.t[64:128, c * 128 + 64:c * 128 + 128], src[2 * c + 1], [], [b])
        cp("act", dst[:], b.t[:, 0:512].rearrange("p (g d) -> p g d", g=4), [b], [nm])
    for half, ksrc in enumerate((key1, key2)):
        b = stage(ksrc.rearrange("h k d -> k h d"), 1024, a=8)
        kb = AR.get([128, 1024], BF16)
        cp("act", kb.t, b.t[:, 0:1024], [b], [kb])
        for h in range(8):
            m = 2 * h + half
            bi = h // 4
            mm(bk[bi][:, (h % 4) * 128:(h % 4 + 1) * 128], kb.t[:, h * 128:(h + 1) * 128], ident[:], True, True,
               [kb, "ident"], [BK[bi]])
        for bi in range(2):
            for q in range(4):
                h = bi * 4 + q
                cp("dve", keyT_sb[:, 2 * h + half, :], bk[bi][:, q * 128:(q + 1) * 128], [BK[bi]], ["keyT_sb"])

    AR.reset()
    stgu = [AR.get([128, 1024], F32) for _ in range(2)]
    wubf = [AR.get([128, 1024], BF16) for _ in range(2)]
    wuTs = [AR.get([128, 8, 128], BF16) for _ in range(2)]
    stgv = [AR.get([128, 1024], F32) for _ in range(2)]
    wvbf = [AR.get([128, 1024], BF16) for _ in range(2)]
    wuT_dv = wuT_d.rearrange("k d e -> d k e")
    def prep_load(eb):
        i = eb % 2
        rows = slice(eb * 128, (eb + 1) * 128)
        dma("sp", ("stgu", i), stgu[i].t, w_u[rows, :], [], [stgu[i]])
        dma("pool", ("stgv", i), stgv[i].t, w_v[rows, :], [], [stgv[i]])

    prep_load(0)
    for eb in range(128):
        i = eb % 2
        rows = slice(eb * 128, (eb + 1) * 128)
        cp("act", wubf[i].t, stgu[i].t, [stgu[i]], [wubf[i]])
        cp("act", wvbf[i].t, stgv[i].t, [stgv[i]], [wvbf[i]])
        if eb + 1 < 128:
            prep_load(eb + 1)
        for dk in range(8):
            bi = 2 * i + dk // 4
            mm(bk[bi][:, (dk % 4) * 128:(dk % 4 + 1) * 128], wubf[i].t[:, dk * 128:(dk + 1) * 128], ident[:], True, True,
               [wubf[i], "ident"], [BK[bi]])
        for hh in range(2):
            bi = 2 * i + hh
            cp("dve", wuTs[i].t[:, hh * 4:(hh + 1) * 4, :], bk[bi][:].rearrange("p (a b) -> p a b", a=4),
               [BK[bi]], [wuTs[i]])
        dma("sp", ("wuTs", i), wuT_dv[:, :, rows], wuTs[i].t, [wuTs[i]], [("wuTd", eb)], nonc=True)
        dma("pool", ("wvbf", i), wv_d[rows, :], wvbf[i].t, [wvbf[i]], [("wvd", eb)])

    NTS = T_SEQ // 128
    assert NSEQ == 2
    groups = [[("p", 0, j), ("p", 1, j)] for j in range(NTS)] + [[("s", 0, 0)]]

    xcnt = [0]
    split_idx = [0]

    def mixer_tile(kind, s, j, sub):
        first = (j == 0)
        last = (kind == "s") or (j == NTS - 1)
        TV = T_S if kind == "s" else 128
        bb = 4 * sub
        bkL = bk[bb:bb + 4]
        BKL = BK[bb:bb + 4]
        U = Us[sub]
        Uk = "U%d" % sub
        hstate = hstates[sub]
        Hk = "hstate%d" % sub
        stat = stats[sub]
        sk = lambda n: "st%d_%d" % (sub, n)
        XNk = "xn2T%d" % sub
        AR.reset(41 * sub)
        xt = AR.get([128, D], F32)
        junk = AR.get([128, D], BF16)
        xnb = AR.get([128, D], BF16)
        xnT = AR.get([128, 8, 128], BF16)
        s2 = AR.get([128, 4, HP + 128], F32)
        s4 = AR.get([128, 4, HP + 128], F32)
        s8 = AR.get([128, 4, HP + 128], F32)
        s16 = AR.get([128, 4, 128], F32)
        dbf = AR.get([128, 4, 128], BF16)
        xc = AR.get([128, 4, 128], F32)
        xcb = AR.get([128, 4, 128], BF16)
        rr = AR.get([128, 4, 128], F32)
        ig = AR.get([128, 4, 128], F32)
        aa = AR.get([128, 4, 128], F32)
        tA = AR.get([128, 4, 128], F32)
        bb = AR.get([128, 4, 128], F32)
        hs = AR.get([128, 4, 128], F32)
        gg = AR.get([128, 4, 128], F32)
        ymix = AR.get([128, 8, 128], BF16)
        AR.reset(41 * sub)
        e_t = AR.get([128, 16, 128], F32)
        sh = AR.get([128, 16, 128], F32)
        v16 = AR.get([128, 16, 16], F32)
        cand = AR.get([128, 8, 256], F32)
        tmp2 = AR.get([128, 8, 256], F32)
        c16 = AR.get([128, 8, 16], F32)

        xcnt[0] += 1
        if kind == "p":
            dma("sp", ("x", xcnt[0] % 2), xt.t, x_p[s, j * 128:(j + 1) * 128, :], [], [xt])
        else:
            memset("pool", xt.t, 0.0, [xt])
            dma("sp", ("x", xcnt[0] % 2), xt.t[0:T_S, :], x_s, [], [xt])
        act(junk.t, xt.t, AF.Square, [xt], [junk, sk(0)], accum_out=stat[:, 0:1])
        act(stat[:, 1:2], stat[:, 0:1], AF.Sqrt, [sk(0)], [sk(1)], bias=EPS, scale=1.0 / D)
        op("dve", lambda e: e.reciprocal(out=stat[:, 2:3], in_=stat[:, 1:2]), [sk(1)], [sk(2)])
        act(xnb.t, xt.t, AF.Identity, [xt, sk(2)], [xnb], scale=stat[:, 2:3])
        for k in range(8):
            bi = 0 if k < 4 else 1
            mm(bkL[bi][:, (k % 4) * 128:(k % 4 + 1) * 128], xnb.t[:, k * 128:(k + 1) * 128], ident[:], True, True,
               [xnb, "ident"], [BKL[bi]])
        cp("dve", xnT.t[:, 0:4, :], bkL[0][:].rearrange("p (a b) -> p a b", a=4), [BKL[0]], [xnT])
        cp("act", xnT.t[:, 4:8, :], bkL[1][:].rearrange("p (a b) -> p a b", a=4), [BKL[1]], [xnT])
        for m in range(12):
            bi = 1 + m // 4
            for k in range(8):
                mm(bkL[bi][:, (m % 4) * 128:(m % 4 + 1) * 128], w_in_sb[:, k, m * 128:(m + 1) * 128], xnT.t[:, k, :],
                   k == 0, k == 7, [xnT, "w_in_sb"], [BKL[bi]])
        if first and kind == "p":
            memset("pool", U[:, :, 0:HP], 0.0, [Uk])
        elif kind == "s":
            memset("pool", U[:, :, 0:HP], 0.0, [Uk])
            for g in range(4):
                dma("sp", "hist", U[:, g, 1:HP], cpool[:, g * 128:(g + 1) * 128].rearrange("r c -> c r"), [], [Uk], nonc=True)
                dma("sp", "hist", U[:, 4 + g, HP - 3:HP], cconv[:, g * 128:(g + 1) * 128].rearrange("r c -> c r"), [], [Uk], nonc=True)
            dma("sp", "hist", hstate[:], clru.rearrange("o (k p) -> p (o k)", p=128), [], [Hk], nonc=True)
        else:
            cp("pool", U[:, :, 0:HP], U[:, :, 128:128 + HP], [Uk], [Uk])
        if first and kind == "p":
            memset("pool", hstate[:], 0.0, [Hk])
        cp("dve", U[:, 0:4, HP:HP + 128], bkL[1][:].rearrange("p (a b) -> p a b", a=4), [BKL[1]], [Uk])
        cp("act", U[:, 4:8, HP:HP + 128], bkL[2][:].rearrange("p (a b) -> p a b", a=4), [BKL[2]], [Uk])
        act(gg.t, bkL[3][:].rearrange("p (a b) -> p a b", a=4), AF.Gelu_apprx_tanh, [BKL[3]], [gg])
        W = HP + 128
        tt("pool", s2.t[:, :, 1:W], U[:, 0:4, 1:W], U[:, 0:4, 0:W - 1], ALU.add, [Uk], [s2])
        tt("pool", s4.t[:, 1:4, 3:W], s2.t[:, 1:4, 3:W], s2.t[:, 1:4, 1:W - 2], ALU.add, [s2], [s4])
        tt("pool", s8.t[:, 2:4, 7:W], s4.t[:, 2:4, 7:W], s4.t[:, 2:4, 3:W - 4], ALU.add, [s4], [s8])
        tt("pool", s16.t[:, 3, :], s8.t[:, 3, HP:W], s8.t[:, 3, HP - 8:W - 8], ALU.add, [s8], [s16])
        cp("pool", s16.t[:, 0, :], s2.t[:, 0, HP:W], [s2], [s16])
        cp("pool", s16.t[:, 1, :], s4.t[:, 1, HP:W], [s4], [s16])
        cp("pool", s16.t[:, 2, :], s8.t[:, 2, HP:W], [s8], [s16])
        invc = invc0 if (first and kind == "p") else invc1
        tt("dve", s16.t, s16.t, invc[:], ALU.mult, [s16, "invc0", "invc1"], [s16])
        tt("dve", dbf.t, s16.t, U[:, 0:4, HP:W], ALU.subtract, [s16, Uk], [dbf])
        for g in range(4):
            mm(bkL[0][:, g * 128:(g + 1) * 128], w_pool_sb[:, g, :], dbf.t[:, g, :], True, True, [dbf, "w_pool_sb"], [BKL[0]])
        tt("dve", ymix.t[:, 0:4, :], bkL[0][:].rearrange("p (a b) -> p a b", a=4),
           sbap(vecs, V_PSC, [[1, 4], [0, 128]]), ALU.mult, [BKL[0], "vecs"], [ymix])
        cwc = lambda k, c: vecs[:, V_CW + 4 * k + c:V_CW + 4 * k + c + 1]
        for c in range(4):
            ts("dve", xc.t[:, c, :], U[:, 4 + c, HP:W], cwc(3, c), vecs[:, V_CB + c:V_CB + c + 1], ALU.mult, ALU.add,
               [Uk, "vecs"], [xc])
        for k in range(3):
            for c in range(4):
                stt("dve", xc.t[:, c, :], U[:, 4 + c, HP - 3 + k:W - 3 + k], cwc(k, c), xc.t[:, c, :], ALU.mult, ALU.add,
                    [Uk, "vecs", xc], [xc])
        cp("dve", xcb.t, xc.t, [xc], [xcb])
        for c in range(4):
            mm(bkL[1][:, c * 128:(c + 1) * 128], wa_sb[:, c, :], xcb.t[:, c, :], True, True, [xcb, "wa_sb"], [BKL[1]])
        for c in range(4):
            mm(bkL[2][:, c * 128:(c + 1) * 128], wx_sb[:, c, :], xcb.t[:, c, :], True, True, [xcb, "wx_sb"], [BKL[2]])
        for c in range(4):
            act(rr.t[:, c, :], bkL[1][:, c * 128:(c + 1) * 128], AF.Sigmoid, [BKL[1], "vecs"], [rr],
                bias=vecs[:, V_BA + c:V_BA + c + 1], scale=1.0)
        for c in range(4):
            act(ig.t[:, c, :], bkL[2][:, c * 128:(c + 1) * 128], AF.Sigmoid, [BKL[2], "vecs"], [ig],
                bias=vecs[:, V_BX + c:V_BX + c + 1], scale=1.0)
        for c in range(4):
            act(aa.t[:, c, :], rr.t[:, c, :], AF.Exp, [rr, "vecs"], [aa], scale=vecs[:, V_CNEG + c:V_CNEG + c + 1])
        tt("dve", tA.t, aa.t, aa.t, ALU.mult, [aa], [tA])
        ts("dve", tA.t, tA.t, -1.0, 1.0, ALU.mult, ALU.add, [tA], [tA])
        act(tA.t, tA.t, AF.Sqrt, [tA], [tA])
        tt("dve", bb.t, ig.t, xc.t, ALU.mult, [ig, xc], [bb])
        tt("dve", bb.t, bb.t, tA.t, ALU.mult, [bb, tA], [bb])
        for c in range(4):
            op("dve", lambda e, c=c: e.tensor_tensor_scan(out=hs.t[:, c, :], data0=aa.t[:, c, :], data1=bb.t[:, c, :],
                                                          initial=hstate[:, c:c + 1], op0=ALU.mult, op1=ALU.add),
               [aa, bb, Hk], [hs])
        cp("dve", hstate[:], hs.t[:, :, TV - 1], [hs], [Hk])
        tt("dve", ymix.t[:, 4:8, :], hs.t, gg.t, ALU.mult, [hs, gg], [ymix])
        if last:
            if kind == "p":
                d_pool, d_conv, d_lru = npool_p[s], nconv_p[s], nlru_p[s:s + 1, :]
            else:
                d_pool, d_conv, d_lru = npool_s, nconv_s, nlru_s
            for g in range(4):
                dma("sp", "stout", d_pool[:, g * 128:(g + 1) * 128].rearrange("r c -> c r"),
                    U[:, g, HP + TV - 15:HP + TV], [Uk], [], nonc=True)
                dma("sp", "stout", d_conv[:, g * 128:(g + 1) * 128].rearrange("r c -> c r"),
                    U[:, 4 + g, HP + TV - 3:HP + TV], [Uk], [], nonc=True)
            dma("sp", "stout", d_lru.rearrange("o (k p) -> p (o k)", p=128), hstate[:], [Hk], [], nonc=True)
        for half in range(2):
            bi = (0, 3)[half]
            for k in range(8):
                mm(bkL[bi][:], ymix.t[:, k, :], w_out_sb[:, k, half * 512:(half + 1) * 512], k == 0, k == 7,
                   [ymix, "w_out_sb"], [BKL[bi]])
        X1 = x1[sub]
        X1k = "x1_%d" % sub
        tt("dve", X1[:, 0:512], xt.t[:, 0:512], bkL[0][:], ALU.add, [xt, BKL[0]], [X1k])
        tt("dve", X1[:, 512:1024], xt.t[:, 512:1024], bkL[3][:], ALU.add, [xt, BKL[3]], [X1k])
        act(junk.t, X1[:], AF.Square, [X1k], [junk, sk(4)], accum_out=stat[:, 4:5])
        act(stat[:, 5:6], stat[:, 4:5], AF.Sqrt, [sk(4)], [sk(5)], bias=EPS, scale=1.0 / D)
        op("dve", lambda e: e.reciprocal(out=stat[:, 6:7], in_=stat[:, 5:6]), [sk(5)], [sk(6)])
        act(xnb.t, X1[:], AF.Identity, [X1k, sk(6)], [xnb], scale=stat[:, 6:7])
        for k in range(8):
            bi = 1 if k < 4 else 2
            mm(bkL[bi][:, (k % 4) * 128:(k % 4 + 1) * 128], xnb.t[:, k * 128:(k + 1) * 128], ident[:], True, True,
               [xnb, "ident"], [BKL[bi]])
        cp("dve", xn2T[:, 0:4, sub * 128:(sub + 1) * 128], bkL[1][:].rearrange("p (a b) -> p a b", a=4), [BKL[1]], [XNk])
        cp("act", xn2T[:, 4:8, sub * 128:(sub + 1) * 128], bkL[2][:].rearrange("p (a b) -> p a b", a=4), [BKL[2]], [XNk])
        split_idx[0] = len(P._rec)
        qb = [0, 1, 2, 3]
        for m in range(16):
            bi = qb[m // 4]
            for k in range(8):
                mm(bkL[bi][:, (m % 4) * 128:(m % 4 + 1) * 128], w_q_sb[:, k, m * 128:(m + 1) * 128],
                   xn2T[:, k, sub * 128:(sub + 1) * 128], k == 0, k == 7, [XNk, "w_q_sb"], [BKL[bi]])
        for qi, bi in enumerate(qb):
            cp("act" if qi % 2 else "dve", qTg[sub][:, qi * 4:(qi + 1) * 4, :], bkL[bi][:].rearrange("p (a b) -> p a b", a=4),
               [BKL[bi]], ["qTg%d" % sub])
        sbk = [0, 1, 2, 3]
        for m in range(16):
            bi = sbk[m // 4]
            mm(bkL[bi][:, (m % 4) * 128:(m % 4 + 1) * 128], qTg[sub][:, m, :], keyT_sb[:, m, :], True, True,
               ["qTg%d" % sub, "keyT_sb"], [BKL[bi]])
        E = e_t.t
        Ek = e_t
        MX = mxs[sub]
        MXk = "mxs%d" % sub
        for qi, bi in enumerate(sbk):
            op("dve", lambda e, qi=qi, bi=bi: e.tensor_reduce(out=MX[:, qi * 4:(qi + 1) * 4],
                                                               in_=bkL[bi][:].rearrange("p (a b) -> p a b", a=4),
                                                               axis=AX.X, op=ALU.max),
               [BKL[bi]], [MXk])
            tt("dve", sh.t[:, qi * 4:(qi + 1) * 4, :], bkL[bi][:].rearrange("p (a b) -> p a b", a=4),
               sbap(MX, qi * 4, [[1, 4], [0, 128]]), ALU.subtract, [BKL[bi], MXk], [sh])
        tt("dve", nshift[sub][:], sbap(MX, 0, [[2, 8]]), sbap(MX, 1, [[2, 8]]), ALU.add, [MXk], ["nshift%d" % sub])
        ts("dve", nshift[sub][:], nshift[sub][:], -1.0, None, ALU.mult, None, ["nshift%d" % sub], ["nshift%d" % sub])
        act(E, sh.t, AF.Exp, [sh], [Ek])
        for m in range(16):
            op("dve", lambda e, m=m: e.max(out=v16.t[:, m, 0:8], in_=E[:, m, :]), [Ek], [v16])
        for m in range(16):
            op("dve", lambda e, m=m: e.match_replace(out=sh.t[:, m, :], in_to_replace=v16.t[:, m, 0:8], in_values=E[:, m, :],
                                                    imm_value=-1.0), [Ek, v16], [sh])
        for m in range(16):
            op("dve", lambda e, m=m: e.max(out=v16.t[:, m, 8:16], in_=sh.t[:, m, :]), [sh], [v16])
        voff = v16.t.offset
        tt("dve", cand.t.rearrange("p h (a b) -> p h a b", a=16),
           sbap(AR.f32, voff, [[32, 8], [1, 16], [0, 16]]),
           sbap(AR.f32, voff + 16, [[32, 8], [0, 16], [1, 16]]), ALU.mult, [v16], [cand])
        for h in range(8):
            op("dve", lambda e, h=h: e.max(out=c16.t[:, h, 0:8], in_=cand.t[:, h, :]), [cand], [c16])
        for h in range(8):
            op("dve", lambda e, h=h: e.match_replace(out=tmp2.t[:, h, :], in_to_replace=c16.t[:, h, 0:8], in_values=cand.t[:, h, :],
                                                    imm_value=-1.0), [cand, c16], [tmp2])
        for h in range(8):
            op("dve", lambda e, h=h: e.max(out=c16.t[:, h, 8:16], in_=tmp2.t[:, h, :]), [tmp2], [c16])
        TH = thrm[sub]
        THk = "thrm%d" % sub
        ts("dve", TH[:], c16.t[:, :, 15], 1.0 - 1e-5, None, ALU.mult, None, [c16], [THk])
        tt("dve", tmp2.t, cand.t, sbap(TH, 0, [[1, 8], [0, 256]]), ALU.is_ge, [cand, THk], [tmp2])
        tt("dve", tmp2.t, tmp2.t, cand.t, ALU.mult, [tmp2, cand], [tmp2])
        op("dve", lambda e: e.tensor_reduce(out=stat[:, 8:16], in_=tmp2.t, axis=AX.X, op=ALU.add), [tmp2], [sk(8)])
        op("dve", lambda e: e.reciprocal(out=stat[:, 8:16], in_=stat[:, 8:16]), [sk(8)], [sk(8)])
        if GELU_MODE == "tanh":
            ts("dve", stat[:, 8:16], stat[:, 8:16], 0.5, None, ALU.mult, None, [sk(8)], [sk(8)])
        tt("dve", diag[sub][:], sbap(ident, 0, [[0, 8], [1, 128]]), sbap(stat, 8, [[1, 8], [0, 128]]), ALU.mult,
           ["ident", sk(8)], ["diag%d" % sub])

    def dense_group(grp):
        NS = len(grp)
        NT = NS * 128
        AR.reset()
        wuT_sb = [AR.get([128, 8, 512], BF16) for _ in range(2)]
        wv_sb = [AR.get([128, 4, 1024], BF16) for _ in range(2)]
        Eb = [AR.get([128, 4, 128], F32) for _ in range(4)]
        Mb = [[[AR.get([128, 4, 128], BF16) for h in range(8)] for sub in range(NS)] for _ in range(2)]
        gl = [AR.get([128, 256], F32) for _ in range(2)]
        AT = [AR.get([128, 256], BF16) for _ in range(2)]
        junk = AR.get([128, D], BF16)
        junk2 = AR.get([128, D], BF16)
        cnt = [0]
        units = [(sub, h) for sub in range(NS) for h in range(8)]

        def load_w(cg):
            wi = cg % 2
            dma("sp", ("wuT_sb", wi), wuT_sb[wi].t, wuT_dv[:, :, cg * 512:(cg + 1) * 512],
                [("wuTd", eb) for eb in range(cg * 4, cg * 4 + 4)], [wuT_sb[wi]], nonc=True)
            dma("sp", ("wv_sb", wi), wv_sb[wi].t, wv_d[cg * 512:(cg + 1) * 512, :].rearrange("(j e) n -> e j n", e=128),
                [("wvd", eb) for eb in range(cg * 4, cg * 4 + 4)], [wv_sb[wi]])

        def mask_units(cg, ulist):
            MB = Mb[cg % 2]
            for (sub, h) in ulist:
                k = cnt[0]
                cnt[0] += 1
                lb = 6 + (k % 2)
                Ebuf = Eb[k % 4]
                qk = "qTg%d" % sub
                mm(bk[lb][:], qTg[sub][:, 2 * h, :], sbap(keyT_sb, (2 * h) * 128 + cg * 4, [[1, 4], [0, 128]]), True, False,
                   [qk, "keyT_sb"], [BK[lb]])
                mm(bk[lb][:], qTg[sub][:, 2 * h + 1, :], sbap(keyT_sb, (2 * h + 1) * 128, [[0, 4], [1, 128]]), False, True,
                   [qk, "keyT_sb"], [BK[lb]])
                act(Ebuf.t, bk[lb][:].rearrange("p (a b) -> p a b", a=4), AF.Exp, [BK[lb], "nshift%d" % sub], [Ebuf],
                    bias=nshift[sub][:, h:h + 1], scale=1.0)
                stt("dve", MB[sub][h].t, Ebuf.t, thrm[sub][:, h:h + 1], Ebuf.t, ALU.is_ge, ALU.mult,
                    [Ebuf, "thrm%d" % sub], [MB[sub][h]])

        C_G = 0.044715
        K_G = 0.7978845608028654

        def chunk_front(cg, j):
            wi = cg % 2
            MB = Mb[cg % 2]
            c = cg * 4 + j
            pi = c % 2
            UB = [BK[4][pi]]
            GB = [BK[5][pi]]
            ut = bk[4][:, pi * 256:pi * 256 + NT]
            g = gl[pi]
            gt_ = g.t[:, 0:NT]
            for dk in range(8):
                mm(ut, wuT_sb[wi].t[:, dk, j * 128:(j + 1) * 128], xn2T[:, dk, 0:NT], dk == 0, dk == 7,
                   [wuT_sb[wi], "xn2T0", "xn2T1"], UB)
            for sub in range(NS):
                for h in range(8):
                    mm(bk[5][:, pi * 256 + sub * 128:pi * 256 + (sub + 1) * 128], MB[sub][h].t[:, j, :], diag[sub][:, h, :],
                       h == 0, h == 7, [MB[sub][h], "diag%d" % sub], GB)
            if GELU_MODE == "tanh":
                act(gt_, ut, AF.Square, UB, [g], scale=float(np.sqrt(C_G)))
                stt("dve", gt_, gt_, 1.0, ut, ALU.add, ALU.mult, [g, UB], [g])
                act(gt_, gt_, AF.Tanh, [g], [g], scale=K_G)
                stt("dve", gt_, gt_, 1.0, ut, ALU.add, ALU.mult, [g, UB], [g])
            else:
                act(gt_, ut, AF.Gelu_apprx_tanh, UB, [g])
            tt("dve", AT[pi].t[:, 0:NT], gt_, bk[5][:, pi * 256:pi * 256 + NT], ALU.mult, [g, GB], [AT[pi]])

        def chunk_back(cg, j):
            wi = cg % 2
            c = cg * 4 + j
            pi = c % 2
            for sub in range(NS):
                for half in range(2):
                    yb = sub * 2 + half
                    mm(bk[yb][:], AT[pi].t[:, sub * 128:(sub + 1) * 128], wv_sb[wi].t[:, j, half * 512:(half + 1) * 512],
                       c == 0, c == 127, [AT[pi], wv_sb[wi]], [BK[yb]])

        load_w(0)
        mask_units(0, units)
        per = (len(units) + 3) // 4
        prev = None
        for cg in range(32):
            for j in range(4):
                chunk_front(cg, j)
                if PIPE_V:
                    if prev is not None:
                        chunk_back(*prev)
                    prev = (cg, j)
                    if j == 0 and cg + 1 < 32:
                        load_w(cg + 1)
                    if cg + 1 < 32:
                        mask_units(cg + 1, units[j * per:(j + 1) * per])
                else:
                    if cg + 1 < 32:
                        mask_units(cg + 1, units[j * per:(j + 1) * per])
                    chunk_back(cg, j)
                    if j == 0 and cg + 1 < 32:
                        load_w(cg + 1)
        if PIPE_V:
            chunk_back(*prev)
        jk = [junk, junk2]
        sc = [(4, 5, 6), (0, 1, 2)]
        subs = list(enumerate(grp))
        for sub, _ in subs:
            X1k = "x1_%d" % sub
            for half in range(2):
                tt("dve", x1[sub][:, half * 512:(half + 1) * 512], x1[sub][:, half * 512:(half + 1) * 512],
                   bk[sub * 2 + half][:], ALU.add, [X1k, BK[sub * 2 + half]], [X1k])
        for sub, _ in subs:
            a, b, c = sc[sub]
            act(jk[sub].t, x1[sub][:], AF.Square, ["x1_%d" % sub], [jk[sub], "ep%d" % a], accum_out=stat2[:, a:a + 1])
        for sub, _ in subs:
            a, b, c = sc[sub]
            act(stat2[:, b:b + 1], stat2[:, a:a + 1], AF.Sqrt, ["ep%d" % a], ["ep%d" % b], bias=EPS, scale=1.0 / D)
        for sub, _ in subs:
            a, b, c = sc[sub]
            op("dve", lambda e, b=b, c=c: e.reciprocal(out=stat2[:, c:c + 1], in_=stat2[:, b:b + 1]), ["ep%d" % b], ["ep%d" % c])
        for sub, (kind, s, j) in subs:
            a, b, c = sc[sub]
            X1k = "x1_%d" % sub
            stt("dve", x1[sub][:], x1[sub][:], stat2[:, c:c + 1], nf_bc[:], ALU.mult, ALU.mult, [X1k, "ep%d" % c, "nf_bc"], [X1k])
            if kind == "p":
                dma("sp", ("yout", sub), y_p[s, j * 128:(j + 1) * 128, :], x1[sub][:], [X1k], [])
            else:
                dma("sp", ("yout", sub), y_s, x1[sub][0:T_S, :], [X1k], [])

    for grp in groups:
        recs = []
        for sub, (kind, s, j) in enumerate(grp):
            P.begin_record()
            mixer_tile(kind, s, j, sub)
            r = P.end_record()
            recs.append((r[:split_idx[0]], r[split_idx[0]:]))
        P.replay(*[r[0] for r in recs])
        P.replay(*[r[1] for r in recs])
        dense_group(grp)

    P.build()
    return nc


_IN_NAMES = ["norm_mix", "w_in", "w_pool", "pool_scale", "conv_w", "conv_b", "lru_wa", "lru_ba", "lru_wx", "lru_bx",
             "lru_lambda", "w_out", "norm_ffn", "peer_wq"]


def run_cores(inputs, T_SEQ, NSEQ, T_S, ncores=NCORES):
    nc = build_program(T_SEQ, NSEQ, T_S)
    f = lambda a: np.ascontiguousarray(np.asarray(a, dtype=np.float32))
    shared = {}
    for n in _IN_NAMES:
        a = f(inputs[n])[0]
        if a.ndim == 1:
            a = a[None, :]
        shared[n] = np.ascontiguousarray(a)
    shared["key1"] = f(inputs["peer_key1"])[0]
    shared["key2"] = f(inputs["peer_key2"])[0]
    shared["w_u"] = f(inputs["peer_wu"])[0]
    shared["w_v"] = f(inputs["peer_wv"])[0]
    shared["norm_final"] = f(inputs["norm_final"])[None, :]
    xp = f(inputs["x_prompt"])
    xs = f(inputs["x_sample"])
    in_maps = []
    for c in range(ncores):
        m = dict(shared)
        m["x_p"] = np.ascontiguousarray(xp[c * NSEQ:(c + 1) * NSEQ])
        m["x_s"] = np.ascontiguousarray(xs[c])
        m["cpool"] = f(inputs["cache_pool"])[0, c]
        m["cconv"] = f(inputs["state_conv"])[0, c]
        m["clru"] = f(inputs["state_lru"])[0, c][None, :]
        in_maps.append(m)
    res = run_bass_kernel_spmd(nc, in_maps, core_ids=list(range(ncores)))
    R = res.results
    cat = lambda k: np.concatenate([np.asarray(r[k]) for r in R], axis=0)
    stk = lambda k: np.stack([np.asarray(r[k]) for r in R], axis=0)
    y_prompt = cat("y_p")
    y_sample = stk("y_s")
    return (y_prompt.astype(np.float32), y_sample.astype(np.float32),
            cat("npool_p")[None].astype(np.float32), cat("nconv_p")[None].astype(np.float32),
            cat("nlru_p")[None].astype(np.float32),
            stk("npool_s")[None].astype(np.float32), stk("nconv_s")[None].astype(np.float32),
            stk("nlru_s")[:, 0, :][None].astype(np.float32))


def kernel(**inputs):
    return run_cores(inputs, 2048, 2, 32)
```
